# Optimizing a Trainium2 kernel written in Bass

```python
import jax, jax.numpy as jnp
from jax import lax
import numpy as np

D_MODEL = 1024
BATCH = 32
SEQ = 2048
DEPTH = 4
DEC_BATCH = 4
DEC_SEQ = 8192
PAST_LEN = 128

N_MIXERS = 2
N_HEADS = 16
N_KV_HEADS = 4
HEAD_DIM = D_MODEL // N_HEADS
KV_GROUP = N_HEADS // N_KV_HEADS
QKV_DIM = (N_HEADS + 2 * N_KV_HEADS) * HEAD_DIM
ROPE_AXIS_DIM = HEAD_DIM // 2
ROPE_THETA = 10000.0
Q_BLOCK = 128
FNET_GROUPS = 8
FNET_GROUP_DIM = D_MODEL // FNET_GROUPS
D_FF = 2816
GRID_W = 64
N_SUBLAYERS = 3
N_MOD = 3
N_ATTN_LAYERS = (DEPTH + 1) // 2
N_FNET_LAYERS = DEPTH // 2
EPS = 1e-6

kernel_name = "hybrid_gqa_fnet_macaron_adaln_encoder"


def rms_norm(x, gain):
    xf = x.astype(jnp.float32)
    y = xf * lax.rsqrt(jnp.mean(xf * xf, axis=-1, keepdims=True) + EPS)
    return (y * gain.astype(jnp.float32)).astype(x.dtype)


def axial_rope_tables(n_tokens):
    rows = n_tokens // GRID_W
    r = jnp.repeat(jnp.arange(rows), GRID_W).astype(jnp.float32)
    c = jnp.tile(jnp.arange(GRID_W), rows).astype(jnp.float32)
    inv = ROPE_THETA ** (-jnp.arange(0, ROPE_AXIS_DIM, 2, dtype=jnp.float32) / ROPE_AXIS_DIM)
    ang = jnp.concatenate([r[:, None] * inv, c[:, None] * inv], axis=-1)
    return jnp.cos(ang), jnp.sin(ang)


def apply_rope(x, cos, sin):
    xf = x.astype(jnp.float32).reshape(*x.shape[:-1], HEAD_DIM // 2, 2)
    x0, x1 = xf[..., 0], xf[..., 1]
    out = jnp.stack([x0 * cos - x1 * sin, x0 * sin + x1 * cos], axis=-1)
    return out.reshape(x.shape).astype(x.dtype)


def gqa_attention(h, w_qkv, q_gain, k_gain, w_o):
    b, s, _ = h.shape
    qkv = h @ w_qkv
    q, k, v = jnp.split(qkv, [N_HEADS * HEAD_DIM, (N_HEADS + N_KV_HEADS) * HEAD_DIM], axis=-1)
    q = q.reshape(b, s, N_KV_HEADS, KV_GROUP, HEAD_DIM)
    k = k.reshape(b, s, N_KV_HEADS, HEAD_DIM)
    v = v.reshape(b, s, N_KV_HEADS, HEAD_DIM)
    cos, sin = axial_rope_tables(s)
    q = apply_rope(rms_norm(q, q_gain), cos[:, None, None, :], sin[:, None, None, :])
    k = apply_rope(rms_norm(k, k_gain), cos[:, None, :], sin[:, None, :])
    scale = 1.0 / np.sqrt(HEAD_DIM)
    n_blk = s // Q_BLOCK
    qb = jnp.moveaxis(q.reshape(b, n_blk, Q_BLOCK, N_KV_HEADS, KV_GROUP, HEAD_DIM), 1, 0)

    def attend(q_blk):
        sc = jnp.einsum('bqkgd,bskd->bkgqs', q_blk, k, preferred_element_type=jnp.float32) * scale
        p = jax.nn.softmax(sc, axis=-1)
        return jnp.einsum('bkgqs,bskd->bqkgd', p.astype(v.dtype), v)

    o = lax.map(attend, qb)
    o = jnp.moveaxis(o, 0, 1).reshape(b, s, N_HEADS * HEAD_DIM)
    return o @ w_o


def fourier_mix(h, w_o):
    b, s, _ = h.shape
    hg = h.astype(jnp.float32).reshape(b, s, FNET_GROUPS, FNET_GROUP_DIM)
    f = jnp.fft.fft2(hg, axes=(1, 3), norm="ortho").real
    return f.reshape(b, s, D_MODEL).astype(h.dtype) @ w_o


def swiglu(h, w_in, w_out):
    g, u = jnp.split(h @ w_in, 2, axis=-1)
    return (jax.nn.silu(g) * u) @ w_out


def run_trunk(x, c, norm_gain, ada_w, ada_b, ffn_w_in, ffn_w_out,
              attn_w_qkv, attn_q_gain, attn_k_gain, attn_w_o, fnet_w_o):
    b = x.shape[0]
    for l in range(DEPTH):
        mod = (jax.nn.silu(c) @ ada_w[l] + ada_b[l]).reshape(b, N_SUBLAYERS, N_MOD, D_MODEL)
        mod = mod[:, :, :, None, :]

        def modnorm(y, j):
            return rms_norm(y, norm_gain[l, j]) * (1.0 + mod[:, j, 1]) + mod[:, j, 0]

        x = x + 0.5 * mod[:, 0, 2] * swiglu(modnorm(x, 0), ffn_w_in[l, 0], ffn_w_out[l, 0])
        h = modnorm(x, 1)
        if l % N_MIXERS == 0:
            a = l // N_MIXERS
            mixed = gqa_attention(h, attn_w_qkv[a], attn_q_gain[a], attn_k_gain[a], attn_w_o[a])
        else:
            mixed = fourier_mix(h, fnet_w_o[l // N_MIXERS])
        x = x + mod[:, 1, 2] * mixed
        x = x + 0.5 * mod[:, 2, 2] * swiglu(modnorm(x, 2), ffn_w_in[l, 1], ffn_w_out[l, 1])
    return x


def setup_inputs(seed: int = 0) -> dict:
    key = jax.random.key(seed)
    ks = jax.random.split(key, 16)
    f32 = jnp.float32
    nrm = lambda k, shape, s: jax.random.normal(k, shape, f32) * s
    return {
        "x_prompt": nrm(ks[0], (BATCH, SEQ, D_MODEL), 1.0),
        "x_sample": nrm(ks[1], (DEC_BATCH, DEC_SEQ, D_MODEL), 1.0),
        "c_prompt": nrm(ks[2], (BATCH, D_MODEL), 1.0),
        "c_sample": nrm(ks[3], (DEC_BATCH, D_MODEL), 1.0),
        "norm_gain": 1.0 + nrm(ks[4], (DEPTH, N_SUBLAYERS, D_MODEL), 0.01),
        "ada_w": nrm(ks[5], (DEPTH, D_MODEL, N_SUBLAYERS * N_MOD * D_MODEL), 0.5 * D_MODEL ** -0.5),
        "ada_b": nrm(ks[6], (DEPTH, N_SUBLAYERS * N_MOD * D_MODEL), 0.01),
        "ffn_w_in": nrm(ks[7], (DEPTH, 2, D_MODEL, 2 * D_FF), D_MODEL ** -0.5),
        "ffn_w_out": nrm(ks[8], (DEPTH, 2, D_FF, D_MODEL), D_FF ** -0.5),
        "attn_w_qkv": nrm(ks[9], (N_ATTN_LAYERS, D_MODEL, QKV_DIM), D_MODEL ** -0.5),
        "attn_q_gain": 1.0 + nrm(ks[10], (N_ATTN_LAYERS, HEAD_DIM), 0.01),
        "attn_k_gain": 1.0 + nrm(ks[11], (N_ATTN_LAYERS, HEAD_DIM), 0.01),
        "attn_w_o": nrm(ks[12], (N_ATTN_LAYERS, N_HEADS * HEAD_DIM, D_MODEL), D_MODEL ** -0.5),
        "fnet_w_o": nrm(ks[13], (N_FNET_LAYERS, D_MODEL, D_MODEL), D_MODEL ** -0.5),
    }


def reference(x_prompt, x_sample, c_prompt, c_sample, norm_gain, ada_w, ada_b, ffn_w_in, ffn_w_out,
              attn_w_qkv, attn_q_gain, attn_k_gain, attn_w_o, fnet_w_o):
    y_prompt = run_trunk(x_prompt, c_prompt, norm_gain, ada_w, ada_b, ffn_w_in, ffn_w_out,
                         attn_w_qkv, attn_q_gain, attn_k_gain, attn_w_o, fnet_w_o)
    y_sample = run_trunk(x_sample, c_sample, norm_gain, ada_w, ada_b, ffn_w_in, ffn_w_out,
                         attn_w_qkv, attn_q_gain, attn_k_gain, attn_w_o, fnet_w_o)
    return (y_prompt, y_sample)
```

```python
import contextlib
import numpy as np
import concourse.bass as bass
import concourse.mybir as mybir
from concourse.bass_utils import run_bass_kernel_spmd

F32 = mybir.dt.float32
BF16 = mybir.dt.bfloat16
AF = mybir.ActivationFunctionType
ALU = mybir.AluOpType

D = 1024
DC = 8
FF = 2816
JC = 22
NB = 1024
NBLK = 12
L = 4
EPS = 1e-6
ENGS = ["sync", "act", "dve", "pool", "pe"]
LO_HEADS = [0, 1, 2, 3, 8, 9, 10, 11]
HI_HEADS = [4, 5, 6, 7, 12, 13, 14, 15]


class Op:
    __slots__ = ("eng", "fn", "deps", "idx", "needed", "tok", "slot", "inc", "waits")


class Prog:
    LIMIT = 60000

    def __init__(self):
        self.ops = {e: [] for e in ENGS}
        self.res = {}
        self.ovl = {}
        self.agg = {}
        self.bufs = {}

    def regbuf(self, name, start, size):
        self.bufs[name] = (start, start + size)
        self.agg[name] = {}
        self.ovl[name] = []
        for o, (s, e) in self.bufs.items():
            if o != name and s < start + size and start < e:
                self.ovl[name].append(o)
                self.ovl[o].append(name)

    def emit(self, eng, fn, reads=(), writes=(), slot=None, inc=None):
        op = Op()
        op.eng = eng
        op.fn = fn
        op.idx = len(self.ops[eng])
        op.needed = False
        op.tok = None
        op.slot = slot
        op.inc = inc
        op.waits = None
        deps = set()
        sk = slot if slot is not None else eng
        for k in reads:
            st = self.res.get(k)
            if st is None:
                st = self.res[k] = [None, {}]
            if st[0] is not None:
                deps.add(st[0])
            for ob in self.ovl.get(k[0], ()):
                deps.update(self.agg[ob].values())
        for k in writes:
            st = self.res.get(k)
            if st is None:
                st = self.res[k] = [None, {}]
            if st[0] is not None:
                deps.add(st[0])
            deps.update(st[1].values())
            for ob in self.ovl.get(k[0], ()):
                deps.update(self.agg[ob].values())
        for k in reads:
            self.res[k][1][sk] = op
            if k[0] in self.agg:
                self.agg[k[0]][sk] = op
        for k in writes:
            self.res[k] = [op, {}]
            if k[0] in self.agg:
                self.agg[k[0]][sk] = op
        deps.discard(op)
        op.deps = deps
        self.ops[eng].append(op)
        return op

    def finalize(self):
        for e in ENGS:
            for op in self.ops[e]:
                keep = []
                for d in op.deps:
                    if d.slot is None and op.slot is None and d.eng == op.eng:
                        if op.eng == "pe":
                            continue
                        if op.idx - d.idx >= 2:
                            continue
                    keep.append(d)
                    d.needed = True
                op.deps = keep
        self.semreq = {}
        self.sloteng = {}
        for e in ENGS:
            cnt = 0
            for op in self.ops[e]:
                if op.slot is not None:
                    assert self.sloteng.setdefault(op.slot, e) == e, op.slot
                    c = self.semreq.get(("slot", op.slot), 0) + (op.inc or 16)
                    self.semreq[("slot", op.slot)] = c
                    op.tok = (("slot", op.slot, 0), c)
                elif op.needed:
                    cnt += 1
                    op.tok = ((e, (cnt - 1) // self.LIMIT), (cnt - 1) % self.LIMIT + 1)
            self.semreq[("eng", e)] = (cnt + self.LIMIT - 1) // self.LIMIT
        for e in ENGS:
            waited = {}
            for op in self.ops[e]:
                w = {}
                for d in op.deps:
                    s, v = d.tok
                    if waited.get(s, 0) >= v:
                        continue
                    if w.get(s, 0) < v:
                        w[s] = v
                for s, v in w.items():
                    waited[s] = v
                op.waits = list(w.items())

    def semkeys(self):
        keys = []
        for k, v in self.semreq.items():
            if k[0] == "slot":
                keys.append(("slot", k[1], 0))
            else:
                for i in range(v):
                    keys.append((k[1], i))
        return keys

    def replay(self, e, eng, sems):
        for op in self.ops[e]:
            for s, v in op.waits:
                eng.wait_ge(sems[s], v)
            ins = op.fn(eng)
            if op.tok is not None:
                if op.slot is not None:
                    ins.then_inc(sems[op.tok[0]], op.inc or 16)
                else:
                    ins.then_inc(sems[op.tok[0]], 1)


def build_program(n_sub=12):
    nc = bass.Bass("TRN2", target_bir_lowering=False)
    P = Prog()

    def din(name, shape, dt=F32):
        return nc.dram_tensor(name, list(shape), dt, kind="ExternalInput").ap()

    x_in = din("x", [NBLK * NB, D])
    cT_in = din("cT", [128, 40])
    gain_in = din("gainT", [128, 96])
    adab_in = din("adabT", [128, 288])
    qkg_in = din("qkg", [128, 4])
    ada_w = din("ada_w", [L, D, 9 * D])
    w_in = din("ffn_w_in", [L, 2, D, 2 * FF])
    w_out = din("ffn_w_out", [L, 2, FF, D])
    w_qkv = din("attn_w_qkv", [2, D, 1536])
    w_ao = din("attn_w_o", [2, D, D])
    w_fo = din("fnet_w_o", [2, D, D])
    ident_in = din("ident", [128, 128])
    perm_in = din("perm", [128, 128])
    ccsc_in = din("ccsc", [128, 256])
    rope_in = din("rope", [6, 128, 2, NB])
    mtab_p_in = din("mtab_p", [16, 128, 3, 128])
    mtab_s_in = din("mtab_s", [64, 128, 3, 128])
    bd_p_in = din("bd_p", [128, 2, 128])
    bd_s_in = din("bd_s", [128, 2, 64])
    y_out = nc.dram_tensor("y", [NBLK * NB, D], F32, kind="ExternalOutput").ap()

    def dscr(name, shape, dt):
        return nc.dram_tensor(name, list(shape), dt)

    xs_t = dscr("xs", [NBLK, 128, DC * NB], F32)
    qs_t = dscr("qs", [NBLK, 128, DC * NB], BF16)
    kp_t = dscr("kseq_p", [4, 128, 2 * 2048], BF16)
    vp_t = dscr("vseq_p", [4, 2048, 512], BF16)
    ko_t = dscr("ks_own", [128, 2 * 4096], BF16)
    kg_t = dscr("ks_gat", [256, 2 * 4096], BF16)
    vo_t = [dscr("vs_own%d" % i, [4096, 256], BF16) for i in range(2)]
    vg_t = [dscr("vs_gat%d" % i, [8192, 256], BF16) for i in range(2)]
    zp_t = dscr("zseq_p", [4, 2048, 2048], BF16)
    zo_t = [dscr("zs_own%d" % i, [4096, 256], BF16) for i in range(8)]
    zg_t = [dscr("zs_gat%d" % i, [8192, 256], BF16) for i in range(8)]
    yp_t = dscr("yseq_p", [4, 2048, 2048], BF16)
    ys_t = dscr("yseq_s", [8192, 2048], BF16)
    wis = dscr("wis", [L * 2 * JC, 128, 2048], BF16).ap()
    wos = dscr("wos", [L * 2 * 8, 128, 2816], BF16).ap()
    xs, qs = xs_t.ap(), qs_t.ap()
    kp, vp, ko, kg = kp_t.ap(), vp_t.ap(), ko_t.ap(), kg_t.ap()
    vo, vg = [t.ap() for t in vo_t], [t.ap() for t in vg_t]
    zo, zg = [t.ap() for t in zo_t], [t.ap() for t in zg_t]
    zp, yp, ysq = zp_t.ap(), yp_t.ap(), ys_t.ap()

    off = [16512]

    def sb(name, shape, dt, at=None, reg=True):
        nbytes = int(np.prod(shape[1:])) * (4 if dt == F32 else 2)
        if at is None:
            at = off[0]
            off[0] = (at + nbytes + 31) // 32 * 32
        t = nc.alloc_sbuf_tensor_at(name, list(shape), dt, offset=at)
        if reg:
            P.regbuf(name, at, nbytes)
        return t

    modT = sb("modT", [128, L * 72 * 5], F32)
    gsT = sb("gsT", [128, L * 3 * 8 * 5], F32)
    hgT = sb("hgT", [128, L * 3 * 8 * 5], F32)
    gainT = sb("gainT", [128, 96], F32)
    adabT = sb("adabT", [128, 288], F32)
    qkg = sb("qkg", [128, 4], F32)
    siluc = sb("siluc", [128, 40], F32)
    siluc_bf = sb("siluc_bf", [128, 40], BF16)
    epst = sb("epst", [128, 1], F32)
    ident = sb("ident", [128, 128], F32)
    perm = sb("perm", [128, 128], F32)
    ones_bf = sb("ones_bf", [128, 128], BF16)
    ones_f = sb("ones_f", [128, 128], F32)
    bones_bf = sb("bones_bf", [128, 128], BF16)
    ccsc_f = sb("ccsc_f", [128, 256], F32)
    ccsc = sb("ccsc", [128, 256], BF16)
    ws = [sb("ws%d" % i, [128, 2816], F32) for i in range(2)]
    wb = [sb("wb%d" % i, [128, 2816], BF16) for i in range(3)]
    sq = [sb("sq%d" % i, [128, 512], BF16) for i in range(2)]
    sqf = sb("sqf", [128, 512], F32, at=P.bufs["sq0"][0])
    ft = [sb("ft%d" % i, [128, 512], F32) for i in range(6)]
    bdp = sb("bdp", [128, 2 * 128], BF16)
    bds = sb("bds", [128, 2 * 64], BF16)
    R0 = off[0]
    xb = [sb("xb%d" % i, [128, DC * NB], F32, at=R0 + i * 32768) for i in range(2)]
    hT = sb("hT", [128, DC * NB], BF16, at=R0 + 65536)
    RA = R0 + 65536 + 16384
    act = sb("act", [128, JC * NB], BF16, at=RA)
    ropeb = sb("ropeb", [128, 2 * NB], F32, at=RA)
    qblk = sb("qblk", [128, DC * NB], BF16, at=RA + 8192)
    kblk = sb("kblk", [128, 2 * NB], BF16, at=RA + 8192 + 16384)
    vblk = sb("vblk", [128, 8 * 512], BF16, at=RA + 8192 + 16384 + 4096)
    ex = [sb("ex%d" % i, [128, 512], F32, at=RA + 36864 + i * 2048) for i in range(3)]
    sq2 = sb("sq2", [128, 512], BF16, at=RA + 36864 + 6144)
    kT = sb("kT", [128, 2 * 8192], BF16, at=R0)
    Vt = sb("Vt", [128, 64 * 512], BF16, at=R0 + 32768)
    qT = sb("qT", [128, DC * NB], BF16, at=R0 + 98304)
    oT = sb("oT", [128, DC * NB], BF16, at=R0 + 114688)
    pr = [sb("pr%d" % i, [128, 512], BF16, at=R0 + 131072 + i * 1024) for i in range(8)]
    rc = [ft[0], ft[1]]
    ZT = sb("ZT", [128, 16 * NB], BF16, at=RA)
    zw = sb("zw", [128, 8 * 2048], BF16, at=R0)
    Ztile = [sb("Ztile%d" % i, [128, 4 * 2048], BF16, at=R0 + i * 16384) for i in range(2)]
    Ytile = [sb("Ytile%d" % i, [128, 4 * 2048], BF16, at=R0 + 32768 + i * 16384) for i in range(2)]
    mtile = [sb("mtile%d" % i, [128, 4 * 3 * 128], BF16, at=R0 + 65536 + i * 3072) for i in range(2)]
    xseq = sb("xseq", [128, DC * 4096], F32, at=R0)
    assert R0 + 131072 + 2 * 4096 <= 229376, R0
    Ykh = [sb("Ykh%d" % i, [128, 2048], BF16, at=R0 + 131072 + i * 4096) for i in range(2)]
    stack = contextlib.ExitStack()
    ps = [stack.enter_context(nc.psum_tensor("ps%d" % i, [128, 512], F32)) for i in range(8)]

    def dma(q, out, in_, reads, writes, slot):
        return P.emit(q, lambda e, o=out, i=in_: e.dma_start(out=o, in_=i), reads, writes, slot=slot)

    def seq_of(blk):
        return blk // 2 if blk < 8 else 4

    def mcol(l, j, t, c, b):
        return ((l * 72) + j * 24 + t * 8 + c) * 5 + b

    def gcol(l, j, c, b):
        return ((l * 3 + j) * 8 + c) * 5 + b

    wcnt = [0]

    def wchunk(loads, ncols):
        i = wcnt[0]
        wcnt[0] += 1
        s2, s3 = i % 2, i % 3
        N = len(loads)
        for n, (dfn, src) in enumerate(loads):
            wk = [("ws%d" % s2, q) for q in range(4) if q % N == n]
            dma("sync", dfn(ws[s2]), src, [], wk, slot="ws%d_%d" % (s2, n))
        P.emit("pool", lambda e, a=wb[s3], b=ws[s2], n=ncols: e.tensor_copy(out=a[:, 0:n], in_=b[:, 0:n]),
               [("ws%d" % s2, q) for q in range(4)], [("wb%d" % s3, 0)])
        return s3

    for (t, src, nm) in [(gainT, gain_in, "gainT"), (adabT, adab_in, "adabT"), (qkg, qkg_in, "qkg"),
                         (siluc, cT_in, "siluc"), (ident, ident_in, "ident"), (perm, perm_in, "perm"),
                         (ccsc_f, ccsc_in, "ccsc_f")]:
        dma("sync", t[:], src[:, :], [], [(nm, 0)], slot="init_" + nm)
    P.emit("dve", lambda e: e.memset(epst[:], EPS), [], [("epst", 0)])
    P.emit("dve", lambda e: e.memset(ones_bf[:], 1.0), [], [("ones_bf", 0)])
    P.emit("dve", lambda e: e.memset(ones_f[:], 1.0), [], [("ones_f", 0)])
    P.emit("dve", lambda e: e.memset(bones_bf[:], 0.0), [], [("bones_bf", 0)])
    P.emit("dve", lambda e: e.memset(bones_bf[0:64, 0:64], 1.0 / 64), [("bones_bf", 0)], [("bones_bf", 1)])
    P.emit("dve", lambda e: e.memset(bones_bf[64:128, 64:128], 1.0 / 64), [("bones_bf", 1)], [("bones_bf", 2)])
    P.emit("dve", lambda e: e.tensor_copy(out=ccsc[:], in_=ccsc_f[:]), [("ccsc_f", 0)], [("ccsc", 0)])
    P.emit("act", lambda e: e.activation(out=siluc_bf[:], in_=siluc[:], func=AF.Silu), [("siluc", 0)], [("siluc", 1)])
    dma("pool", bdp[:], bd_p_in.rearrange("p a n -> p (a n)"), [], [("bdp", 0)], slot="init_bdp")
    dma("pool", bds[:], bd_s_in.rearrange("p a n -> p (a n)"), [], [("bds", 0)], slot="init_bds")
    CONST_R = [("epst", 0), ("ones_bf", 0), ("bones_bf", 2), ("ccsc", 0), ("siluc", 1), ("ident", 0), ("perm", 0),
               ("gainT", 0), ("adabT", 0), ("qkg", 0), ("bdp", 0), ("bds", 0)]

    def adaln():
        for l in range(L):
            for hc in range(36):
                wi = wchunk([(lambda t: t[:, 0:2048].rearrange("p (k c) -> p k c", k=8),
                              ada_w[l, :, hc * 256:(hc + 1) * 256].rearrange("(k p) c -> p k c", p=128))], 2048)
                wv = wb[wi][:, 0:2048].rearrange("p (k c) -> p k c", k=8)
                for mi in range(2):
                    m = hc * 2 + mi
                    bk = m % 2
                    for kc in range(8):
                        P.emit("pe", lambda e, o=ps[bk][:, m * 5:m * 5 + 5], w=wv[:, kc, mi * 128:(mi + 1) * 128],
                               r=siluc_bf[:, kc * 5:kc * 5 + 5], st=(kc == 0), sp=(kc == 7):
                               e.matmul(o, w, r, start=st, stop=sp),
                               [("wb%d" % wi, 0), ("siluc", 1)], [("ps", bk)])
                    P.emit("act", lambda e, o=modT[:, (l * 72 + m) * 5:(l * 72 + m) * 5 + 5],
                           i=ps[bk][:, m * 5:m * 5 + 5], b=adabT[:, l * 72 + m:l * 72 + m + 1]:
                           e.activation(out=o, in_=i, func=AF.Identity, bias=b, scale=1.0),
                           [("ps", bk), ("adabT", 0)], [("modT", l)])
        for l in range(L):
            for j in range(3):
                for c in range(8):
                    P.emit("dve", lambda e, o=gsT[:, gcol(l, j, c, 0):gcol(l, j, c, 0) + 5],
                           i=modT[:, mcol(l, j, 1, c, 0):mcol(l, j, 1, c, 0) + 5],
                           g=gainT[:, (l * 3 + j) * 8 + c:(l * 3 + j) * 8 + c + 1]:
                           e.tensor_scalar(out=o, in0=i, scalar1=1.0, scalar2=g, op0=ALU.add, op1=ALU.mult),
                           [("modT", l), ("gainT", 0)], [("gsT", 0)])
                    P.emit("dve", lambda e, o=hgT[:, gcol(l, j, c, 0):gcol(l, j, c, 0) + 5],
                           i=modT[:, mcol(l, j, 2, c, 0):mcol(l, j, 2, c, 0) + 5], f=(1.0 if j == 1 else 0.5):
                           e.tensor_scalar(out=o, in0=i, scalar1=f, scalar2=None, op0=ALU.mult),
                           [("modT", l)], [("hgT", 0)])

    evq = [0]

    def evac(out, in_, reads, writes):
        evq[0] += 1
        if evq[0] % 2:
            return P.emit("act", lambda e: e.activation(out=out, in_=in_, func=AF.Copy), reads, writes)
        return P.emit("dve", lambda e: e.tensor_copy(out=out, in_=in_), reads, writes)

    pbank = [0]

    def nextbank(lo=4, n=4):
        pbank[0] += 1
        return lo + pbank[0] % n

    def ingest(blk):
        stg = xb[1][:, :].rearrange("p (t f) -> p t f", t=8)
        dma("sync", stg, x_in[blk * NB:(blk + 1) * NB, :].rearrange("(t p) f -> p t f", p=128), [], [("xb1", 0)], slot="xb1")
        dst = xb[0][:, :].rearrange("p (c n) -> p c n", c=8)
        for c in range(8):
            for th in range(2):
                b = nextbank()
                for t4 in range(4):
                    tcn = th * 4 + t4
                    P.emit("pe", lambda e, o=ps[b][:, t4 * 128:(t4 + 1) * 128], i=stg[:, tcn, c * 128:(c + 1) * 128]:
                           e.transpose(o, i, ident[:]), [("xb1", 0)], [("ps", b)])
                evac(dst[:, c, th * 512:(th + 1) * 512], ps[b][:, :], [("ps", b)], [("xb0", c)])
        dma("sync", xs[blk], xb[0][:, :], [("xb0", c) for c in range(8)], [("xs", blk)], slot="st_xb0")

    def egress(blk):
        dma("sync", xb[0][:, :], xs[blk], [("xs", blk)], [("xb0", c) for c in range(8)], slot="xb0")
        src = xb[0][:, :].rearrange("p (c n) -> p c n", c=8)
        stg = xb[1][:, :].rearrange("p (t f) -> p t f", t=8)
        for tcn in range(8):
            for fh in range(2):
                b = nextbank()
                for c4 in range(4):
                    c = fh * 4 + c4
                    P.emit("pe", lambda e, o=ps[b][:, c4 * 128:(c4 + 1) * 128], i=src[:, c, tcn * 128:(tcn + 1) * 128]:
                           e.transpose(o, i, ident[:]), [("xb0", c)], [("ps", b)])
                evac(stg[:, tcn, fh * 512:(fh + 1) * 512], ps[b][:, :], [("ps", b)], [("xb1", tcn)])
        dma("sync", y_out[blk * NB:(blk + 1) * NB, :].rearrange("(t p) f -> p t f", p=128), stg,
            [("xb1", t) for t in range(8)], [("y", blk)], slot="st_xb1")

    def modnorm(xt, xname, l, j, b):
        xv = xt[:, :].rearrange("p (c n) -> p c n", c=8)
        hv = hT[:, :].rearrange("p (c n) -> p c n", c=8)
        for tt in range(NB // 512):
            sl = slice(tt * 512, (tt + 1) * 512)
            bank = tt % 2
            for c in range(8):
                P.emit("act", lambda e, o=sq[c % 2][:, :], i=xv[:, c, sl]: e.activation(out=o, in_=i, func=AF.Square),
                       [(xname, c)], [("sq%d" % (c % 2), 0)])
                P.emit("pe", lambda e, o=ps[bank][:, :], r=sq[c % 2][:, :], st=(c == 0), sp=(c == 7):
                       e.matmul(o, ones_bf[:], r, start=st, stop=sp), [("sq%d" % (c % 2), 0)], [("ps", bank)])
            rs = ft[tt]
            P.emit("act", lambda e, o=rs[:, :], i=ps[bank][:, :]: e.activation(out=o, in_=i, func=AF.Sqrt, bias=epst[:, 0:1], scale=1.0 / D),
                   [("ps", bank)], [("ft%d" % tt, 0)])
            P.emit("dve", lambda e, o=rs[:, :]: e.reciprocal(out=o, in_=o), [("ft%d" % tt, 0)], [("ft%d" % tt, 0)])
            for c in range(8):
                tmp = ft[2 + c % 2]
                P.emit("dve", lambda e, o=tmp[:, :], a=xv[:, c, sl], r=rs[:, :]: e.tensor_tensor(out=o, in0=a, in1=r, op=ALU.mult),
                       [(xname, c), ("ft%d" % tt, 0)], [("ft%d" % (2 + c % 2), 0)])
                P.emit("act", lambda e, o=hv[:, c, sl], i=tmp[:, :], s=gsT[:, gcol(l, j, c, b):gcol(l, j, c, b) + 1],
                       bb=modT[:, mcol(l, j, 0, c, b):mcol(l, j, 0, c, b) + 1]:
                       e.activation(out=o, in_=i, func=AF.Identity, bias=bb, scale=s),
                       [("ft%d" % (2 + c % 2), 0)], [("hT", c, tt)])

    def ffn_load_stats(blk, l, s, xi):
        xt, xname = xb[xi], "xb%d" % xi
        dma("sync", xt[:, :], xs[blk], [("xs", blk)], [(xname, c) for c in range(8)], slot=xname)
        xv = xt[:, :].rearrange("p (c n) -> p c n", c=8)
        accs, accn = [ft[2], ft[3]], ["ft2", "ft3"]
        tmp = sq[0][:, :].bitcast(F32) if False else None
        for tt in range(NB // 512):
            sl = slice(tt * 512, (tt + 1) * 512)
            acc, an = accs[tt], accn[tt]
            for c in range(8):
                if c == 0:
                    P.emit("pool", lambda e, o=acc[:, :], a_=xv[:, c, sl]: e.tensor_tensor(out=o, in0=a_, in1=a_, op=ALU.mult),
                           [(xname, c)], [(an, 0)])
                else:
                    P.emit("pool", lambda e, o=sqf[:, :], a_=xv[:, c, sl]: e.tensor_tensor(out=o, in0=a_, in1=a_, op=ALU.mult),
                           [(xname, c)], [("sqf", 0)])
                    P.emit("pool", lambda e, o=acc[:, :], t_=sqf[:, :]: e.tensor_tensor(out=o, in0=o, in1=t_, op=ALU.add),
                           [(an, 0), ("sqf", 0)], [(an, 0)])

    def ffn_stats2(blk, l, s, xi):
        accs, accn = [ft[2], ft[3]], ["ft2", "ft3"]
        for tt in range(NB // 512):
            acc, an = accs[tt], accn[tt]
            bank = tt % 2
            P.emit("pe", lambda e, o=ps[bank][:, :], r=acc[:, :]: e.matmul(o, ones_f[:], r, start=True, stop=True),
                   [(an, 0)], [("ps", bank)])
            rs = ft[tt]
            P.emit("act", lambda e, o=rs[:, :], i=ps[bank][:, :]: e.activation(out=o, in_=i, func=AF.Sqrt, bias=epst[:, 0:1], scale=1.0 / D),
                   [("ps", bank)], [("ft%d" % tt, 0)])
            P.emit("dve", lambda e, o=rs[:, :]: e.reciprocal(out=o, in_=o), [("ft%d" % tt, 0)], [("ft%d" % tt, 0)])

    def ffn_apply(blk, l, s, xi):
        b = seq_of(blk)
        j = 0 if s == 0 else 2
        xt, xname = xb[xi], "xb%d" % xi
        xv = xt[:, :].rearrange("p (c n) -> p c n", c=8)
        hv = hT[:, :].rearrange("p (c n) -> p c n", c=8)
        for tt in range(NB // 512):
            sl = slice(tt * 512, (tt + 1) * 512)
            rs = ft[tt]
            for c in range(8):
                tmp = ft[2 + c % 2]
                P.emit("dve", lambda e, o=tmp[:, :], a_=xv[:, c, sl], r=rs[:, :]: e.tensor_tensor(out=o, in0=a_, in1=r, op=ALU.mult),
                       [(xname, c), ("ft%d" % tt, 0)], [("ft%d" % (2 + c % 2), 0)])
                P.emit("act", lambda e, o=hv[:, c, sl], i=tmp[:, :], s_=gsT[:, gcol(l, j, c, b):gcol(l, j, c, b) + 1],
                       bb=modT[:, mcol(l, j, 0, c, b):mcol(l, j, 0, c, b) + 1]:
                       e.activation(out=o, in_=i, func=AF.Identity, bias=bb, scale=s_),
                       [("ft%d" % (2 + c % 2), 0)], [("hT", c, tt)])

    def wbf(dram_ap, ncols, key):
        i = wcnt[0]
        wcnt[0] += 1
        s3 = i % 3
        dma("sync", wb[s3][:, 0:ncols], dram_ap, [key], [("wb%d" % s3, 0)], slot="wbl%d" % s3)
        return s3

    def ffn_a(blk, l, s, xi, first, hook=None):
        hv = hT[:, :].rearrange("p (c n) -> p c n", c=8)
        av = act[:, :].rearrange("p (j n) -> p j n", j=JC)
        for jc in range(JC):
            if jc == 3 and hook is not None:
                hook[0]()
            if jc == 14 and hook is not None:
                hook[1]()
            widx = (l * 2 + s) * JC + jc
            if first:
                wi = wchunk([
                    (lambda t: t[:, 0:2048].rearrange("p (k c) -> p k c", k=8)[:, :, 0:128],
                     w_in[l, s, :, jc * 128:(jc + 1) * 128].rearrange("(k p) c -> p k c", p=128)),
                    (lambda t: t[:, 0:2048].rearrange("p (k c) -> p k c", k=8)[:, :, 128:256],
                     w_in[l, s, :, FF + jc * 128:FF + (jc + 1) * 128].rearrange("(k p) c -> p k c", p=128))], 2048)
                dma("sync", wis[widx], wb[wi][:, 0:2048], [("wb%d" % wi, 0)], [("wis", widx)], slot="st_wb%d" % wi)
            else:
                wi = wbf(wis[widx], 2048, ("wis", widx))
            wv = wb[wi][:, 0:2048].rearrange("p (k c) -> p k c", k=8)
            for tt in range(NB // 512):
                sl = slice(tt * 512, (tt + 1) * 512)
                bg, bu = 2 + tt, 4 + tt
                for kc in range(8):
                    P.emit("pe", lambda e, o=ps[bg][:, :], w=wv[:, kc, 0:128], r=hv[:, kc, sl], st=(kc == 0), sp=(kc == 7):
                           e.matmul(o, w, r, start=st, stop=sp), [("wb%d" % wi, 0), ("hT", kc, tt)], [("ps", bg)])
                for kc in range(8):
                    P.emit("pe", lambda e, o=ps[bu][:, :], w=wv[:, kc, 128:256], r=hv[:, kc, sl], st=(kc == 0), sp=(kc == 7):
                           e.matmul(o, w, r, start=st, stop=sp), [("wb%d" % wi, 0), ("hT", kc, tt)], [("ps", bu)])
                sg = ft[4 + tt]
                P.emit("act", lambda e, o=sg[:, :], i=ps[bg][:, :]: e.activation(out=o, in_=i, func=AF.Silu),
                       [("ps", bg)], [("ft%d" % (4 + tt), 0)])
                P.emit("dve", lambda e, o=av[:, jc, sl], a=ps[bu][:, :], g=sg[:, :]: e.tensor_tensor(out=o, in0=a, in1=g, op=ALU.mult),
                       [("ps", bu), ("ft%d" % (4 + tt), 0)], [("act", jc, tt)])

    def ffn_b(blk, l, s, xi, first):
        b = seq_of(blk)
        j = 0 if s == 0 else 2
        xt, xname = xb[xi], "xb%d" % xi
        xv = xt[:, :].rearrange("p (c n) -> p c n", c=8)
        av = act[:, :].rearrange("p (j n) -> p j n", j=JC)
        for f in range(8):
            widx = (l * 2 + s) * 8 + f
            if first:
                wi = wchunk([(lambda t: t[:, 0:2816].rearrange("p (k c) -> p k c", k=JC),
                              w_out[l, s, :, f * 128:(f + 1) * 128].rearrange("(k p) c -> p k c", p=128))], 2816)
                dma("sync", wos[widx], wb[wi][:, 0:2816], [("wb%d" % wi, 0)], [("wos", widx)], slot="st_wb%d" % wi)
            else:
                wi = wbf(wos[widx], 2816, ("wos", widx))
            wv = wb[wi][:, 0:2816].rearrange("p (k c) -> p k c", k=JC)
            for tt in range(NB // 512):
                sl = slice(tt * 512, (tt + 1) * 512)
                bo = 6 + tt
                for jc in range(JC):
                    P.emit("pe", lambda e, o=ps[bo][:, :], w=wv[:, jc, :], r=av[:, jc, sl], st=(jc == 0), sp=(jc == JC - 1):
                           e.matmul(o, w, r, start=st, stop=sp), [("wb%d" % wi, 0), ("act", jc, tt)], [("ps", bo)])
                P.emit("dve", lambda e, o=xv[:, f, sl], a=ps[bo][:, :], g=hgT[:, gcol(l, j, f, b):gcol(l, j, f, b) + 1]:
                       e.scalar_tensor_tensor(out=o, in0=a, scalar=g, in1=o, op0=ALU.mult, op1=ALU.add),
                       [("ps", bo), (xname, f)], [(xname, f)])
        dma("act", xs[blk], xt[:, :], [(xname, c) for c in range(8)], [("xs", blk)], slot="sta_" + xname)

    xi_state = [0]

    def ffn_pass(l, s):
        order = [8, 9, 10, 11] + list(range(8))
        xi = xi_state[0]
        ffn_load_stats(order[0], l, s, xi)
        ffn_stats2(order[0], l, s, xi)
        ffn_apply(order[0], l, s, xi)
        for n, blk in enumerate(order):
            if n + 1 < len(order):
                ffn_a(blk, l, s, xi, n == 0, hook=((lambda nb=order[n + 1], x2=xi ^ 1: ffn_load_stats(nb, l, s, x2)),
                                                   (lambda nb=order[n + 1], x2=xi ^ 1: ffn_stats2(nb, l, s, x2))))
                ffn_apply(order[n + 1], l, s, xi ^ 1)
            else:
                ffn_a(blk, l, s, xi, n == 0)
            ffn_b(blk, l, s, xi, n == 0)
            xi ^= 1
        xi_state[0] = xi

    def attn1(blk, l):
        a = l // 2
        b = seq_of(blk)
        rtype = (blk % 2) if blk < 8 else 2 + (blk - 8)
        xt, xname = xb[1], "xb1"
        dma("sync", xt[:, :], xs[blk], [("xs", blk)], [(xname, c) for c in range(8)], slot=xname)
        dma("sync", ropeb[:, :], rope_in[rtype].rearrange("p a n -> p (a n)"), [], [("ropeb", 0)], slot="ropeb")
        modnorm(xt, xname, l, 1, b)
        hv = hT[:, :].rearrange("p (c n) -> p c n", c=8)
        qv = qblk[:, :].rearrange("p (c n) -> p c n", c=8)
        kv = kblk[:, :].rearrange("p (c n) -> p c n", c=2)
        tiles = [(ci, tt) for ci in range(10) for tt in range(2)]
        x1b, x1n = [ft[0], ft[1], ft[2]], ["ft0", "ft1", "ft2"]
        sqb, sqn = [sq[0], sq[1], sq2], ["sq0", "sq1", "sq2"]
        rsb, rsn = [ft[3], ft[4]], ["ft3", "ft4"]
        t1b, t1n = [ft[5], ex[0]], ["ft5", "ex0"]
        t2b, t2n = [ex[1], ex[2]], ["ex1", "ex2"]
        wstate = {}

        def stage_a(i):
            ci, tt = tiles[i]
            if tt == 0:
                if ci < 8:
                    aa, bb = ci // 4, ci % 4
                    cl = 512 * aa + 64 * bb
                    ch = cl + 256
                    loads = [(lambda t: t[:, 0:1024].rearrange("p (k c) -> p k c", k=8)[:, :, 0:64],
                              w_qkv[a, :, cl:cl + 64].rearrange("(k p) c -> p k c", p=128)),
                             (lambda t: t[:, 0:1024].rearrange("p (k c) -> p k c", k=8)[:, :, 64:128],
                              w_qkv[a, :, ch:ch + 64].rearrange("(k p) c -> p k c", p=128))]
                else:
                    c0 = 1024 + (ci - 8) * 128
                    loads = [(lambda t: t[:, 0:1024].rearrange("p (k c) -> p k c", k=8),
                              w_qkv[a, :, c0:c0 + 128].rearrange("(k p) c -> p k c", p=128))]
                wstate[ci] = wchunk(loads, 1024)
            wi = wstate[ci]
            gcolq = a * 2 if ci < 8 else a * 2 + 1
            wv = wb[wi][:, 0:1024].rearrange("p (k c) -> p k c", k=8)
            sl = slice(tt * 512, (tt + 1) * 512)
            bq = i % 3
            for kc in range(8):
                P.emit("pe", lambda e, o=ps[bq][:, :], w=wv[:, kc, :], r=hv[:, kc, sl], st=(kc == 0), sp=(kc == 7):
                       e.matmul(o, w, r, start=st, stop=sp), [("wb%d" % wi, 0), ("hT", kc, tt)], [("ps", bq)])
            x1 = x1b[i % 3]
            P.emit("act", lambda e, o=x1[:, :], i_=ps[bq][:, :], g=qkg[:, gcolq:gcolq + 1]:
                   e.activation(out=o, in_=i_, func=AF.Identity, scale=g), [("ps", bq)], [(x1n[i % 3], 0)])
            P.emit("act", lambda e, o=sqb[i % 3][:, :], i_=ps[bq][:, :]: e.activation(out=o, in_=i_, func=AF.Square),
                   [("ps", bq)], [(sqn[i % 3], 0)])

        def stage_b(i):
            bm, br = 3 + i % 2, 5 + i % 3
            P.emit("pe", lambda e, o=ps[bm][:, :], r=sqb[i % 3][:, :]: e.matmul(o, bones_bf[:], r, start=True, stop=True),
                   [(sqn[i % 3], 0)], [("ps", bm)])
            P.emit("pe", lambda e, o=ps[br][:, :], r=x1b[i % 3][:, :]: e.matmul(o, perm[:], r, start=True, stop=True),
                   [(x1n[i % 3], 0)], [("ps", br)])
            rs = rsb[i % 2]
            P.emit("act", lambda e, o=rs[:, :], i_=ps[bm][:, :]: e.activation(out=o, in_=i_, func=AF.Sqrt, bias=epst[:, 0:1], scale=1.0),
                   [("ps", bm)], [(rsn[i % 2], 0)])

        def stage_c(i):
            ci, tt = tiles[i]
            sl = slice(tt * 512, (tt + 1) * 512)
            br = 5 + i % 3
            rs, x1, t1, t2 = rsb[i % 2], x1b[i % 3], t1b[i % 2], t2b[i % 2]
            P.emit("dve", lambda e, o=rs[:, :]: e.reciprocal(out=o, in_=o), [(rsn[i % 2], 0)], [(rsn[i % 2], 0)])
            P.emit("dve", lambda e, o=t1[:, :], a_=x1[:, :], c=ropeb[:, sl]: e.tensor_tensor(out=o, in0=a_, in1=c, op=ALU.mult),
                   [(x1n[i % 3], 0), ("ropeb", 0)], [(t1n[i % 2], 0)])
            P.emit("dve", lambda e, o=t2[:, :], a_=ps[br][:, :], c=ropeb[:, NB + tt * 512:NB + (tt + 1) * 512]:
                   e.tensor_tensor(out=o, in0=a_, in1=c, op=ALU.mult), [("ps", br), ("ropeb", 0)], [(t2n[i % 2], 0)])
            P.emit("dve", lambda e, o=t1[:, :], a_=t1[:, :], c=t2[:, :]: e.tensor_tensor(out=o, in0=a_, in1=c, op=ALU.add),
                   [(t1n[i % 2], 0), (t2n[i % 2], 0)], [(t1n[i % 2], 0)])
            dst = qv[:, ci, sl] if ci < 8 else kv[:, ci - 8, sl]
            dkey = ("qblk", ci) if ci < 8 else ("kblk", ci - 8)
            P.emit("dve", lambda e, o=dst, a_=t1[:, :], c=rs[:, :]: e.tensor_tensor(out=o, in0=a_, in1=c, op=ALU.mult),
                   [(t1n[i % 2], 0), (rsn[i % 2], 0)], [dkey])

        nt = len(tiles)
        for step in range(nt + 2):
            if step < nt:
                stage_a(step)
            if 0 <= step - 1 < nt:
                stage_b(step - 1)
            if 0 <= step - 2 < nt:
                stage_c(step - 2)
        wi = wchunk([(lambda t: t[:, 0:2048].rearrange("p (k c) -> p k c", k=8),
                      w_qkv[a, :, 1280:1536].rearrange("(k p) c -> p k c", p=128))], 2048)
        wv = wb[wi][:, 0:2048].rearrange("p (k c) -> p k c", k=8)
        vv = vblk[:, :].rearrange("p (t g d) -> p t g d", t=8, g=4)
        P.emit("pool", lambda e, o=vv[:, :, :, 64:128]: e.memset(o, 1.0), [], [("vblk", "ones")])
        for tcn in range(8):
            bv = nextbank(2, 2)
            for kc in range(8):
                P.emit("pe", lambda e, o=ps[bv][:, 0:256], w=hv[:, kc, tcn * 128:(tcn + 1) * 128], r=wv[:, kc, :], st=(kc == 0), sp=(kc == 7):
                       e.matmul(o, w, r, start=st, stop=sp), [("wb%d" % wi, 0), ("hT", kc, tcn // 4)], [("ps", bv)])
            evac(vv[:, tcn, :, 0:64], ps[bv][:, 0:256].rearrange("p (g d) -> p g d", g=4), [("ps", bv)], [("vblk", tcn)])
        dma("sync", qs[blk], qblk[:, :], [("qblk", c) for c in range(8)], [("qs", blk)], slot="st_qblk")
        if blk < 8:
            sq_, hf = blk // 2, blk % 2
            kdst = kp[sq_].rearrange("p (c n) -> p c n", c=2)[:, :, hf * NB:(hf + 1) * NB]
            vdst = vp[sq_, hf * NB:(hf + 1) * NB, :].rearrange("(t p) d -> p t d", p=128)
            kkey, vkey = ("kp", sq_, hf), ("vp", sq_, hf)
        else:
            sb_ = blk - 8
            kdst = ko.rearrange("p (c n) -> p c n", c=2)[:, :, sb_ * NB:(sb_ + 1) * NB]
            vdst = None
            kkey, vkey = ("ko", sb_), ("vo", sb_)
        dma("sync", kdst, kv, [("kblk", 0), ("kblk", 1)], [kkey], slot="st_kblk")
        vrd = [("vblk", t) for t in range(8)] + [("vblk", "ones")]
        v3 = vblk[:, :].rearrange("p (t d) -> p t d", t=8)
        if vdst is not None:
            dma("sync", vdst, v3, vrd, [vkey], slot="st_vblk0")
        else:
            for i in range(2):
                dma("sync", vo[i][sb_ * NB:(sb_ + 1) * NB, :].rearrange("(t p) d -> p t d", p=128), v3[:, :, i * 256:(i + 1) * 256],
                    vrd, [("vo", sb_, i)], slot="st_vblk%d" % i)

    def gather(kind):
        RG = [[0, 1], [2, 3], [4, 5], [6, 7]]
        if kind == "kv":
            P.emit("pool", lambda e: e.collective_compute("AllGather", ALU.bypass, replica_groups=RG,
                                                          ins=[ko_t.ap().opt()], outs=[kg_t.ap().opt()]),
                   [("ko", i) for i in range(4)], [("kg", 0)], slot="cc_k", inc=1)
            for j in range(2):
                P.emit("pool", lambda e, j=j: e.collective_compute("AllGather", ALU.bypass, replica_groups=RG,
                                                                   ins=[vo_t[j].ap().opt()], outs=[vg_t[j].ap().opt()]),
                       [("vo", i, j) for i in range(4)], [("vg", j)], slot="cc_v%d" % j, inc=1)
        else:
            for j in range(8):
                P.emit("pool", lambda e, j=j: e.collective_compute("AllGather", ALU.bypass, replica_groups=RG,
                                                                   ins=[zo_t[j].ap().opt()], outs=[zg_t[j].ap().opt()]),
                       [("zo", i, j) for i in range(4)], [("zg", j)], slot="cc_z%d" % j, inc=1)

    def attn2(blk, l, load_kv):
        a = l // 2
        b = seq_of(blk)
        S = 2048 if blk < 8 else 8192
        NSC = S // 128
        kTv = kT[:, 0:2 * S].rearrange("p (c n) -> p c n", c=2)
        Vv = Vt[:, 0:NSC * 512].rearrange("p (t d) -> p t d", t=NSC)
        if load_kv:
            if blk < 8:
                sq_ = blk // 2
                dma("sync", kT[:, 0:2 * S], kp[sq_], [("kp", sq_, 0), ("kp", sq_, 1)], [("kT", 0)], slot="kT")
                dma("sync", Vv, vp[sq_].rearrange("(t p) d -> p t d", p=128), [("vp", sq_, 0), ("vp", sq_, 1)], [("Vt", 0), ("Vt", 1)], slot="Vt0")
            else:
                dma("sync", kT[:, :].rearrange("p (c r n) -> p c r n", c=2, r=2), kg.rearrange("(r p) (c n) -> p c r n", r=2, c=2),
                    [("kg", 0)], [("kT", 0)], slot="kT")
                for i in range(2):
                    dma("sync", Vv[:, :, i * 256:(i + 1) * 256], vg[i].rearrange("(t p) d -> p t d", p=128), [("vg", i)], [("Vt", i)], slot="Vt%d" % i)
        dma("sync", qT[:, :], qs[blk], [("qs", blk)], [("qT", 0)], slot="qT")
        qv = qT[:, :].rearrange("p (c n) -> p c n", c=8)
        ov = oT[:, :].rearrange("p (c n) -> p c n", c=8)
        scale = 0.125
        pcnt = 0
        for ci in range(8):
            aa, bb = ci // 4, ci % 4
            heads = [(8 * aa + bb, 0), (8 * aa + 4 + bb, 64)]
            for qt in range(NB // 512):
                sl = slice(qt * 512, (qt + 1) * 512)
                ob = [4 + ((ci * 2 + qt) % 2) * 2, 5 + ((ci * 2 + qt) % 2) * 2]

                def qk(sc):
                    for hi, (h, p0) in enumerate(heads):
                        g = h // 4
                        bs = hi * 2 + sc % 2
                        P.emit("pe", lambda e, o=ps[bs][:, :], w=kTv[p0:p0 + 64, g // 2, sc * 128:(sc + 1) * 128], r=qv[p0:p0 + 64, ci, sl]:
                               e.matmul(o, w, r, start=True, stop=True), [("kT", 0), ("qT", 0)], [("ps", bs)])
                qk(0)
                for sc in range(NSC):
                    if sc + 1 < NSC:
                        qk(sc + 1)
                    for hi, (h, p0) in enumerate(heads):
                        g = h // 4
                        bs = hi * 2 + sc % 2
                        pi = pcnt % 8
                        pcnt += 1
                        P.emit("act", lambda e, o=pr[pi][:, :], i=ps[bs][:, :]: e.activation(out=o, in_=i, func=AF.Exp, scale=scale),
                               [("ps", bs)], [("pr%d" % pi, 0)])
                        P.emit("pe", lambda e, o=ps[ob[hi]][:, :], sc=sc, g=g, r=pr[pi][:, :], st=(sc == 0), sp=(sc == NSC - 1):
                               e.matmul(o, Vt[:, sc * 512 + g * 128:sc * 512 + (g + 1) * 128], r, start=st, stop=sp),
                               [("Vt", 0), ("Vt", 1), ("pr%d" % pi, 0)], [("ps", ob[hi])])
                for hi, (h, p0) in enumerate(heads):
                    r_ = rc[hi]
                    P.emit("dve", lambda e, o=r_[0:64, :], i=ps[ob[hi]][64:128, :]: e.reciprocal(out=o, in_=i),
                           [("ps", ob[hi])], [("ft%d" % hi, 0)])
                    P.emit("dve", lambda e, o=ov[p0:p0 + 64, ci, sl], a=ps[ob[hi]][0:64, :], c=r_[0:64, :]:
                           e.tensor_tensor(out=o, in0=a, in1=c, op=ALU.mult), [("ps", ob[hi]), ("ft%d" % hi, 0)], [("oT", ci, hi, qt)])
        xt, xname = xb[0], "xb0"
        dma("sync", xt[:, :], xs[blk], [("xs", blk)], [(xname, c) for c in range(8)], slot=xname)
        xv = xt[:, :].rearrange("p (c n) -> p c n", c=8)
        for f in range(8):
            wsrc = w_ao[a].rearrange("(a h b p) n -> p a h b n", a=2, h=2, b=4, p=64)
            lds = []
            for h_ in range(2):
                for a_ in range(2):
                    lds.append((lambda t, h_=h_, a_=a_: t[h_ * 64:(h_ + 1) * 64, a_ * 512:(a_ + 1) * 512].rearrange("p (b c) -> p b c", b=4),
                                wsrc[:, a_, h_, :, f * 128:(f + 1) * 128]))
            wi = wchunk(lds, 1024)
            wv = wb[wi][:, 0:1024].rearrange("p (k c) -> p k c", k=8)
            for tt in range(NB // 512):
                sl = slice(tt * 512, (tt + 1) * 512)
                bo = nextbank(0, 4)
                for ci in range(8):
                    P.emit("pe", lambda e, o=ps[bo][:, :], w=wv[:, ci, :], r=ov[:, ci, sl], st=(ci == 0), sp=(ci == 7):
                           e.matmul(o, w, r, start=st, stop=sp), [("wb%d" % wi, 0), ("oT", ci, 0, tt), ("oT", ci, 1, tt)], [("ps", bo)])
                P.emit("dve", lambda e, o=xv[:, f, sl], a=ps[bo][:, :], g=hgT[:, gcol(l, 1, f, b):gcol(l, 1, f, b) + 1]:
                       e.scalar_tensor_tensor(out=o, in0=a, scalar=g, in1=o, op0=ALU.mult, op1=ALU.add),
                       [("ps", bo), (xname, f)], [(xname, f)])
        dma("sync", xs[blk], xt[:, :], [(xname, c) for c in range(8)], [("xs", blk)], slot="st_" + xname)

    def vaug(Vt_, sc, g):
        base = Vt_[:, sc * 320 + g * 64: sc * 320 + g * 64 + 64]
        full = Vt_[:, sc * 320 + g * 64: sc * 320 + 320]
        n = (256 - 64 * g) // 64
        v3 = full.rearrange("p (a d) -> p a d", d=64)
        return v3[:, 0:n + 1:n, :]

    def fnet1(blk, l):
        a = l // 2
        b = seq_of(blk)
        xt, xname = xb[1], "xb1"
        dma("sync", xt[:, :], xs[blk], [("xs", blk)], [(xname, c) for c in range(8)], slot=xname)
        modnorm(xt, xname, l, 1, b)
        hv = hT[:, :].rearrange("p (c n) -> p c n", c=8)
        Zv = ZT[:, :].rearrange("p (c n) -> p c n", c=16)
        for g in range(8):
            for tt in range(NB // 512):
                sl = slice(tt * 512, (tt + 1) * 512)
                for part in range(2):
                    bz = nextbank(2, 4)
                    P.emit("pe", lambda e, o=ps[bz][:, :], w=ccsc[:, part * 128:(part + 1) * 128], r=hv[:, g, sl]:
                           e.matmul(o, w, r, start=True, stop=True), [("hT", g, tt)], [("ps", bz)])
                    evac(Zv[:, part * 8 + g, sl], ps[bz][:, :], [("ps", bz)], [("ZT", part * 8 + g, tt)])
        zv = zw[:, :].rearrange("p (t f) -> p t f", t=8)
        for cg in range(4):
            wi = wchunk([(lambda t: t[:, 0:2048].rearrange("p (k c) -> p k c", k=8),
                          w_fo[a, :, cg * 256:(cg + 1) * 256].rearrange("(k p) c -> p k c", p=128))], 2048)
            wv = wb[wi][:, 0:2048].rearrange("p (k c) -> p k c", k=8)
            for tcn in range(8):
                for part in range(2):
                    bz = nextbank(2, 4)
                    for dc in range(8):
                        P.emit("pe", lambda e, o=ps[bz][:, 0:256], w=Zv[:, part * 8 + dc, tcn * 128:(tcn + 1) * 128], r=wv[:, dc, :], st=(dc == 0), sp=(dc == 7):
                               e.matmul(o, w, r, start=st, stop=sp), [("wb%d" % wi, 0), ("ZT", part * 8 + dc, tcn // 4)], [("ps", bz)])
                    evac(zv[:, tcn, part * 1024 + cg * 256:part * 1024 + (cg + 1) * 256], ps[bz][:, 0:256], [("ps", bz)], [("zw", tcn, part, cg)])
        if blk < 8:
            sq_, hf = blk // 2, blk % 2
            zdst = zp[sq_, hf * NB:(hf + 1) * NB, :].rearrange("(t p) f -> p t f", p=128)
            zkey = ("zp", sq_, hf)
        else:
            sb_ = blk - 8
            zdst = None
        zrd = [("zw", t, p_, c) for t in range(8) for p_ in range(2) for c in range(4)]
        if zdst is not None:
            dma("sync", zdst, zv, zrd, [zkey], slot="st_zw0")
        else:
            for i in range(8):
                dma("sync", zo[i][sb_ * NB:(sb_ + 1) * NB, :].rearrange("(t p) f -> p t f", p=128), zv[:, :, i * 256:(i + 1) * 256],
                    zrd, [("zo", sb_, i)], slot="st_zw%d" % i)

    def fnet2(seq):
        sample = seq == 4
        Q = 64 if sample else 16
        mt_in = mtab_s_in if sample else mtab_p_in
        for grp in range(Q // 4):
            i2 = grp % 2
            Zv = Ztile[i2][:, :].rearrange("p (q f) -> p q f", q=4)
            if sample:
                for i in range(8):
                    src = zg[i].rearrange("(n1 n2) f -> n1 n2 f", n2=64)[:, grp * 4:(grp + 1) * 4, :]
                    dma("sync", Zv[:, :, i * 256:(i + 1) * 256], src, [("zg", i)], [("Ztile%d" % i2, i)], slot="Ztile%d_%d" % (i2, i))
            else:
                src = zp[seq].rearrange("(n1 n2) f -> n1 n2 f", n2=16)[:, grp * 4:(grp + 1) * 4, :]
                dma("sync", Zv, src, [("zp", seq, 0), ("zp", seq, 1)], [("Ztile%d" % i2, i) for i in range(8)], slot="Ztile%d_0" % i2)
            mv = mtile[i2][:, :].rearrange("p (q a n) -> p q a n", q=4, a=3)
            dma("pool", mv, mt_in[grp * 4:(grp + 1) * 4].rearrange("q p a n -> p q a n"), [], [("mtile%d" % i2, 0)], slot="mtile%d" % i2)
            Yv = Ytile[i2][:, :].rearrange("p (q f) -> p q f", q=4)
            for q in range(4):
                for fh in range(2):
                    A_ = Zv[:, q, fh * 512:(fh + 1) * 512]
                    B_ = Zv[:, q, 1024 + fh * 512:1024 + (fh + 1) * 512]
                    br_, bi_ = nextbank(0, 4), nextbank(4, 4)
                    rd = [("Ztile%d" % i2, i) for i in range(8)] + [("mtile%d" % i2, 0)]
                    P.emit("pe", lambda e, o=ps[br_][:, :], w=mv[:, q, 0, :], r=A_: e.matmul(o, w, r, start=True, stop=False), rd, [("ps", br_)])
                    P.emit("pe", lambda e, o=ps[br_][:, :], w=mv[:, q, 1, :], r=B_: e.matmul(o, w, r, start=False, stop=True), rd, [("ps", br_)])
                    P.emit("pe", lambda e, o=ps[bi_][:, :], w=mv[:, q, 1, :], r=A_: e.matmul(o, w, r, start=True, stop=False), rd, [("ps", bi_)])
                    P.emit("pe", lambda e, o=ps[bi_][:, :], w=mv[:, q, 2, :], r=B_: e.matmul(o, w, r, start=False, stop=True), rd, [("ps", bi_)])
                    evac(Yv[:, q, fh * 512:(fh + 1) * 512], ps[br_][:, :], [("ps", br_)], [("Ytile%d" % i2, q, fh, 0)])
                    evac(Yv[:, q, 1024 + fh * 512:1024 + (fh + 1) * 512], ps[bi_][:, :], [("ps", bi_)], [("Ytile%d" % i2, q, fh, 1)])
            ydram = ysq if sample else yp[seq]
            ydst = ydram.rearrange("(k1 n2) f -> k1 n2 f", n2=Q)[:, grp * 4:(grp + 1) * 4, :]
            dma("sync", ydst, Yv, [("Ytile%d" % i2, q, fh, ri) for q in range(4) for fh in range(2) for ri in range(2)],
                [("ysq", grp) if sample else ("yp", seq, grp)], slot="st_Ytile%d" % i2)

    def fnet3(seq, l):
        sample = seq == 4
        Q = 64 if sample else 16
        S = 8192 if sample else 2048
        T_own = 4096 if sample else 2048
        NKH = S // 128
        R = 128 // Q
        blks = [8, 9, 10, 11] if sample else [2 * seq, 2 * seq + 1]
        xv = xseq[:, 0:DC * T_own].rearrange("p (c n) -> p c n", c=8)
        for bi_, blk in enumerate(blks):
            dma("sync", xv[:, :, bi_ * NB:(bi_ + 1) * NB], xs[blk].rearrange("p (c n) -> p c n", c=8), [("xs", blk)], [("xseq", bi_)], slot="xseq%d" % bi_)
        bd = bds if sample else bdp
        NCOL = 64 if sample else 128
        bdv = bd[:, :].rearrange("p (a n) -> p a n", a=2)
        KB = 512 // NCOL
        ydram = ysq if sample else yp[seq]
        yrd = [("ysq", g) for g in range(Q // 4)] if sample else [("yp", seq, g) for g in range(Q // 4)]
        nk2 = NCOL // R
        for khg in range(NKH // KB):
            for kb in range(KB):
                kh = khg * KB + kb
                yi = kh % 2
                dma("sync", Ykh[yi][:, :], ydram[kh * 128:(kh + 1) * 128, :], yrd, [("Ykh%d" % yi, 0)], slot="Ykh%d" % yi)
                for f in range(8):
                    bo = f
                    o = ps[bo][:, kb * NCOL:(kb + 1) * NCOL]
                    P.emit("pe", lambda e, o=o, w=Ykh[yi][:, f * 128:(f + 1) * 128], r=bdv[:, 0, :]: e.matmul(o, w, r, start=True, stop=False),
                           [("Ykh%d" % yi, 0)], [("ps", bo)])
                    P.emit("pe", lambda e, o=o, w=Ykh[yi][:, 1024 + f * 128:1024 + (f + 1) * 128], r=bdv[:, 1, :]: e.matmul(o, w, r, start=False, stop=True),
                           [("Ykh%d" % yi, 0)], [("ps", bo)])
            for f in range(8):
                pv = ps[f][:, :].rearrange("p (m k) -> p m k", k=nk2)
                t0 = khg * KB * R
                xo = xv[:, f, :].rearrange("p (k m) -> p m k", m=128)[:, t0:t0 + KB * R, :]
                P.emit("dve", lambda e, o=xo, a=pv, g=hgT[:, gcol(l, 1, f, seq):gcol(l, 1, f, seq) + 1]:
                       e.scalar_tensor_tensor(out=o, in0=a, scalar=g, in1=o, op0=ALU.mult, op1=ALU.add),
                       [("ps", f)] + [("xseq", i) for i in range(len(blks))], [("xseq", "u", f)])
        for bi_, blk in enumerate(blks):
            dma("sync", xs[blk].rearrange("p (c n) -> p c n", c=8), xv[:, :, bi_ * NB:(bi_ + 1) * NB],
                [("xseq", "u", f) for f in range(8)] + [("xseq", bi_)], [("xs", blk)], slot="st_xseq%d" % bi_)

    adaln()
    for blk in range(NBLK):
        ingest(blk)
    sub = 0
    xi = 0
    done = False
    for l in range(L):
        if done:
            break
        ffn_pass(l, 0)
        sub += 1
        if sub >= n_sub:
            break
        if l % 2 == 0:
            for blk in [8, 9, 10, 11]:
                attn1(blk, l)
            gather("kv")
            for sq_ in range(4):
                attn1(2 * sq_, l)
                attn1(2 * sq_ + 1, l)
                attn2(2 * sq_, l, True)
                attn2(2 * sq_ + 1, l, True)
            for i, blk in enumerate([8, 9, 10, 11]):
                attn2(blk, l, True)
        else:
            for blk in [8, 9, 10, 11]:
                fnet1(blk, l)
            gather("z")
            for sq_ in range(4):
                fnet1(2 * sq_, l)
                fnet1(2 * sq_ + 1, l)
                fnet2(sq_)
                fnet3(sq_, l)
            fnet2(4)
            fnet3(4, l)
        sub += 1
        if sub >= n_sub:
            break
        ffn_pass(l, 1)
        sub += 1
        if sub >= n_sub:
            break
    for blk in range(NBLK):
        egress(blk)

    P.finalize()
    keys = P.semkeys()
    sems = {}
    for i, k in enumerate(keys):
        sems[k] = stack.enter_context(nc.semaphore("s%d" % i))
    with stack:
        with nc.Block() as block:
            @block.sync
            def _(e):
                P.replay("sync", e, sems)
                for k in keys:
                    if k[0] == "slot":
                        e.wait_ge(sems[k], P.semreq[("slot", k[1])])

            @block.scalar
            def _(e):
                P.replay("act", e, sems)

            @block.vector
            def _(e):
                P.replay("dve", e, sems)

            @block.gpsimd
            def _(e):
                P.replay("pool", e, sems)

            @block.tensor
            def _(e):
                P.replay("pe", e, sems)
    return nc, P


_CACHE = {}


def _consts(par):
    c = {}
    c["ident"] = np.eye(128, dtype=np.float32)
    pm = np.zeros((128, 128), np.float32)
    for i in range(64):
        pm[2 * i + 1, 2 * i] = -1.0
        pm[2 * i, 2 * i + 1] = 1.0
    c["perm"] = pm
    cd = np.arange(128)[:, None] * np.arange(128)[None, :]
    ang = 2 * np.pi * cd / 128.0
    c["ccsc"] = np.concatenate([np.cos(ang), np.sin(ang)], axis=1).astype(np.float32)
    inv = 10000.0 ** (-np.arange(0, 32, 2, dtype=np.float32) / 32.0)
    rope = np.zeros((6, 128, 2, NB), np.float32)
    for ty in range(6):
        if ty < 2:
            pos = ty * NB + np.arange(NB)
        else:
            pos = par * 4096 + (ty - 2) * NB + np.arange(NB)
        r = (pos // 64).astype(np.float32)
        cc = (pos % 64).astype(np.float32)
        angp = np.concatenate([r[:, None] * inv, cc[:, None] * inv], axis=-1).astype(np.float32)
        for p in range(128):
            pr_ = (p % 64) // 2
            rope[ty, p, 0] = np.cos(angp[:, pr_])
            rope[ty, p, 1] = np.sin(angp[:, pr_])
    c["rope"] = rope
    for nm, Q in (("mtab_p", 16), ("mtab_s", 64)):
        S = 128 * Q
        n1 = np.arange(128)[:, None].astype(np.float64)
        k1 = np.arange(128)[None, :].astype(np.float64)
        mt = np.zeros((Q, 128, 3, 128), np.float32)
        for n2 in range(Q):
            th = 2 * np.pi * (n1 * k1 / 128.0 + n2 * k1 / S)
            mt[n2, :, 0] = np.cos(th)
            mt[n2, :, 1] = -np.sin(th)
            mt[n2, :, 2] = -np.cos(th)
        c[nm] = mt
    bdp = np.zeros((128, 2, 128), np.float64)
    Q, R = 16, 8
    nrm = 1.0 / np.sqrt(2048 * 128.0)
    for kl in range(R):
        for n2 in range(Q):
            for k2 in range(16):
                ph = 2 * np.pi * n2 * k2 / Q
                bdp[kl * Q + n2, 0, kl * 16 + k2] = np.cos(ph) * nrm
                bdp[kl * Q + n2, 1, kl * 16 + k2] = np.sin(ph) * nrm
    c["bd_p"] = bdp.astype(np.float32)
    bds = np.zeros((128, 2, 64), np.float64)
    Q, R = 64, 2
    nrm = 1.0 / np.sqrt(8192 * 128.0)
    for kl in range(R):
        for n2 in range(Q):
            for k2l in range(32):
                k2 = par * 32 + k2l
                ph = 2 * np.pi * n2 * k2 / Q
                bds[kl * Q + n2, 0, kl * 32 + k2l] = np.cos(ph) * nrm
                bds[kl * Q + n2, 1, kl * 32 + k2l] = np.sin(ph) * nrm
    c["bd_s"] = bds.astype(np.float32)
    return c


def kernel(x_prompt, x_sample, c_prompt, c_sample, norm_gain, ada_w, ada_b, ffn_w_in, ffn_w_out,
           attn_w_qkv, attn_q_gain, attn_k_gain, attn_w_o, fnet_w_o, _n_sub=12):
    f = lambda a: np.ascontiguousarray(np.asarray(a, dtype=np.float32))
    x_prompt, x_sample, c_prompt, c_sample = f(x_prompt), f(x_sample), f(c_prompt), f(c_sample)
    norm_gain, ada_w, ada_b = f(norm_gain), f(ada_w), f(ada_b)
    ffn_w_in, ffn_w_out, attn_w_qkv, attn_w_o, fnet_w_o = f(ffn_w_in), f(ffn_w_out), f(attn_w_qkv), f(attn_w_o), f(fnet_w_o)
    qg, kg_ = f(attn_q_gain), f(attn_k_gain)
    key = _n_sub
    if key not in _CACHE:
        _CACHE[key] = build_program(_n_sub)[0]
    nc = _CACHE[key]
    gainT = np.ascontiguousarray(norm_gain.reshape(4, 3, 8, 128).transpose(3, 0, 1, 2).reshape(128, 96))
    adabT = np.ascontiguousarray(ada_b.reshape(4, 72, 128).transpose(2, 0, 1).reshape(128, 288))
    qkg = np.stack([np.tile(qg[0], 2), np.tile(kg_[0], 2), np.tile(qg[1], 2), np.tile(kg_[1], 2)], axis=1).astype(np.float32)
    consts = [_consts(0), _consts(1)]
    in_maps = []
    for c in range(8):
        par = c % 2
        xs_ = np.concatenate([x_prompt[4 * c:4 * c + 4].reshape(8192, D),
                              x_sample[c // 2, par * 4096:(par + 1) * 4096]], axis=0)
        cv = np.concatenate([c_prompt[4 * c:4 * c + 4], c_sample[c // 2:c // 2 + 1]], axis=0)
        cT = np.ascontiguousarray(cv.reshape(5, 8, 128).transpose(2, 1, 0).reshape(128, 40))
        m = {"x": np.ascontiguousarray(xs_), "cT": cT, "gainT": gainT, "adabT": adabT, "qkg": qkg,
             "ada_w": ada_w, "ffn_w_in": ffn_w_in, "ffn_w_out": ffn_w_out, "attn_w_qkv": attn_w_qkv,
             "attn_w_o": attn_w_o, "fnet_w_o": fnet_w_o}
        m.update(consts[par])
        in_maps.append(m)
    res = run_bass_kernel_spmd(nc, in_maps, core_ids=list(range(8)))
    y_prompt = np.zeros((32, 2048, D), np.float32)
    y_sample = np.zeros((4, 8192, D), np.float32)
    for c in range(8):
        y = res.results[c]["y"]
        y_prompt[4 * c:4 * c + 4] = y[0:8192].reshape(4, 2048, D)
        y_sample[c // 2, (c % 2) * 4096:(c % 2 + 1) * 4096] = y[8192:12288]
    return (y_prompt, y_sample)
```

```python
import contextlib
import numpy as np
import concourse.bass as bass
import concourse.mybir as mybir
from concourse.bass_utils import run_bass_kernel_spmd

F32 = mybir.dt.float32
BF16 = mybir.dt.bfloat16
AF = mybir.ActivationFunctionType
ALU = mybir.AluOpType

D = 1024
DC = 8
FF = 2816
JC = 22
NB = 1024
NBLK = 12
L = 4
EPS = 1e-6
ENGS = ["sync", "act", "dve", "pool", "pe"]
LO_HEADS = [0, 1, 2, 3, 8, 9, 10, 11]
HI_HEADS = [4, 5, 6, 7, 12, 13, 14, 15]


class Op:
    __slots__ = ("eng", "fn", "deps", "idx", "needed", "tok", "slot", "inc", "waits")


class Prog:
    LIMIT = 60000

    def __init__(self):
        self.ops = {e: [] for e in ENGS}
        self.res = {}
        self.ovl = {}
        self.agg = {}
        self.bufs = {}

    def regbuf(self, name, start, size):
        self.bufs[name] = (start, start + size)
        self.agg[name] = {}
        self.ovl[name] = []
        for o, (s, e) in self.bufs.items():
            if o != name and s < start + size and start < e:
                self.ovl[name].append(o)
                self.ovl[o].append(name)

    def emit(self, eng, fn, reads=(), writes=(), slot=None, inc=None):
        op = Op()
        op.eng = eng
        op.fn = fn
        op.idx = len(self.ops[eng])
        op.needed = False
        op.tok = None
        op.slot = slot
        op.inc = inc
        op.waits = None
        deps = set()
        sk = slot if slot is not None else eng
        for k in reads:
            st = self.res.get(k)
            if st is None:
                st = self.res[k] = [None, {}]
            if st[0] is not None:
                deps.add(st[0])
            for ob in self.ovl.get(k[0], ()):
                deps.update(self.agg[ob].values())
        for k in writes:
            st = self.res.get(k)
            if st is None:
                st = self.res[k] = [None, {}]
            if st[0] is not None:
                deps.add(st[0])
            deps.update(st[1].values())
            for ob in self.ovl.get(k[0], ()):
                deps.update(self.agg[ob].values())
        for k in reads:
            self.res[k][1][sk] = op
            if k[0] in self.agg:
                self.agg[k[0]][sk] = op
        for k in writes:
            self.res[k] = [op, {}]
            if k[0] in self.agg:
                self.agg[k[0]][sk] = op
        deps.discard(op)
        op.deps = deps
        self.ops[eng].append(op)
        return op

    def finalize(self):
        for e in ENGS:
            for op in self.ops[e]:
                keep = []
                for d in op.deps:
                    if d.slot is None and op.slot is None and d.eng == op.eng:
                        if op.eng == "pe":
                            continue
                        if op.idx - d.idx >= 2:
                            continue
                    keep.append(d)
                    d.needed = True
                op.deps = keep
        self.semreq = {}
        self.sloteng = {}
        for e in ENGS:
            cnt = 0
            for op in self.ops[e]:
                if op.slot is not None:
                    assert self.sloteng.setdefault(op.slot, e) == e, op.slot
                    c = self.semreq.get(("slot", op.slot), 0) + (op.inc or 16)
                    self.semreq[("slot", op.slot)] = c
                    op.tok = (("slot", op.slot, 0), c)
                elif op.needed:
                    cnt += 1
                    op.tok = ((e, (cnt - 1) // self.LIMIT), (cnt - 1) % self.LIMIT + 1)
            self.semreq[("eng", e)] = (cnt + self.LIMIT - 1) // self.LIMIT
        for e in ENGS:
            waited = {}
            for op in self.ops[e]:
                w = {}
                for d in op.deps:
                    s, v = d.tok
                    if waited.get(s, 0) >= v:
                        continue
                    if w.get(s, 0) < v:
                        w[s] = v
                for s, v in w.items():
                    waited[s] = v
                op.waits = list(w.items())

    def semkeys(self):
        keys = []
        for k, v in self.semreq.items():
            if k[0] == "slot":
                keys.append(("slot", k[1], 0))
            else:
                for i in range(v):
                    keys.append((k[1], i))
        return keys

    def replay(self, e, eng, sems):
        for op in self.ops[e]:
            for s, v in op.waits:
                eng.wait_ge(sems[s], v)
            ins = op.fn(eng)
            if op.tok is not None:
                if op.slot is not None:
                    ins.then_inc(sems[op.tok[0]], op.inc or 16)
                else:
                    ins.then_inc(sems[op.tok[0]], 1)


def build_program(n_sub=12):
    nc = bass.Bass("TRN2", target_bir_lowering=False)
    P = Prog()

    def din(name, shape, dt=F32):
        return nc.dram_tensor(name, list(shape), dt, kind="ExternalInput").ap()

    x_in = din("x", [NBLK * NB, D])
    cT_in = din("cT", [128, 40])
    gain_in = din("gainT", [128, 96])
    adab_in = din("adabT", [128, 288])
    qkg_in = din("qkg", [128, 4])
    ada_w = din("ada_w", [L, D, 9 * D])
    w_in = din("ffn_w_in", [L, 2, D, 2 * FF])
    w_out = din("ffn_w_out", [L, 2, FF, D])
    w_qkv = din("attn_w_qkv", [2, D, 1536])
    w_ao = din("attn_w_o", [2, D, D])
    w_fo = din("fnet_w_o", [2, D, D])
    ident_in = din("ident", [128, 128])
    perm_in = din("perm", [128, 128])
    ccsc_in = din("ccsc", [128, 256])
    rope_in = din("rope", [6, 128, 2, NB])
    mtab_p_in = din("mtab_p", [16, 128, 3, 128])
    mtab_s_in = din("mtab_s", [64, 128, 3, 128])
    bd_p_in = din("bd_p", [128, 2, 128])
    bd_s_in = din("bd_s", [128, 2, 64])
    y_out = nc.dram_tensor("y", [NBLK * NB, D], F32, kind="ExternalOutput").ap()

    def dscr(name, shape, dt):
        return nc.dram_tensor(name, list(shape), dt)

    xs_t = dscr("xs", [NBLK, 128, DC * NB], F32)
    qs_t = dscr("qs", [NBLK, 128, DC * NB], BF16)
    kp_t = dscr("kseq_p", [4, 128, 2 * 2048], BF16)
    vp_t = dscr("vseq_p", [4, 2048, 512], BF16)
    ko_t = dscr("ks_own", [128, 2 * 4096], BF16)
    kg_t = dscr("ks_gat", [256, 2 * 4096], BF16)
    vo_t = [dscr("vs_own%d" % i, [4096, 256], BF16) for i in range(2)]
    vg_t = [dscr("vs_gat%d" % i, [8192, 256], BF16) for i in range(2)]
    zp_t = dscr("zseq_p", [4, 2048, 2048], BF16)
    zo_t = [dscr("zs_own%d" % i, [4096, 256], BF16) for i in range(8)]
    zg_t = [dscr("zs_gat%d" % i, [8192, 256], BF16) for i in range(8)]
    yp_t = dscr("yseq_p", [4, 2048, 2048], BF16)
    ys_t = dscr("yseq_s", [8192, 2048], BF16)
    wis = dscr("wis", [L * 2 * JC, 128, 2048], BF16).ap()
    wos = dscr("wos", [L * 2 * 8, 128, 2816], BF16).ap()
    xs, qs = xs_t.ap(), qs_t.ap()
    kp, vp, ko, kg = kp_t.ap(), vp_t.ap(), ko_t.ap(), kg_t.ap()
    vo, vg = [t.ap() for t in vo_t], [t.ap() for t in vg_t]
    zo, zg = [t.ap() for t in zo_t], [t.ap() for t in zg_t]
    zp, yp, ysq = zp_t.ap(), yp_t.ap(), ys_t.ap()

    off = [16512]

    def sb(name, shape, dt, at=None, reg=True):
        nbytes = int(np.prod(shape[1:])) * (4 if dt == F32 else 2)
        if at is None:
            at = off[0]
            off[0] = (at + nbytes + 31) // 32 * 32
        t = nc.alloc_sbuf_tensor_at(name, list(shape), dt, offset=at)
        if reg:
            P.regbuf(name, at, nbytes)
        return t

    modT = sb("modT", [128, L * 72 * 5], F32)
    gsT = sb("gsT", [128, L * 3 * 8 * 5], F32)
    hgT = sb("hgT", [128, L * 3 * 8 * 5], F32)
    gainT = sb("gainT", [128, 96], F32)
    adabT = sb("adabT", [128, 288], F32)
    qkg = sb("qkg", [128, 4], F32)
    siluc = sb("siluc", [128, 40], F32)
    siluc_bf = sb("siluc_bf", [128, 40], BF16)
    epst = sb("epst", [128, 1], F32)
    ident = sb("ident", [128, 128], F32)
    perm = sb("perm", [128, 128], F32)
    ones_bf = sb("ones_bf", [128, 128], BF16)
    ones_f = sb("ones_f", [128, 128], F32)
    bones_bf = sb("bones_bf", [128, 128], BF16)
    ccsc_f = sb("ccsc_f", [128, 256], F32)
    ccsc = sb("ccsc", [128, 256], BF16)
    ws = [sb("ws%d" % i, [128, 2816], F32) for i in range(2)]
    wb = [sb("wb%d" % i, [128, 2816], BF16) for i in range(3)]
    sq = [sb("sq%d" % i, [128, 512], BF16) for i in range(2)]
    sqf = sb("sqf", [128, 512], F32, at=P.bufs["sq0"][0])
    ft = [sb("ft%d" % i, [128, 512], F32) for i in range(6)]
    bdp = sb("bdp", [128, 2 * 128], BF16)
    bds = sb("bds", [128, 2 * 64], BF16)
    R0 = off[0]
    xb = [sb("xb%d" % i, [128, DC * NB], F32, at=R0 + i * 32768) for i in range(2)]
    hT = sb("hT", [128, DC * NB], BF16, at=R0 + 65536)
    RA = R0 + 65536 + 16384
    act = sb("act", [128, JC * NB], BF16, at=RA)
    ropeb = sb("ropeb", [128, 2 * NB], F32, at=RA)
    qblk = sb("qblk", [128, DC * NB], BF16, at=RA + 8192)
    kblk = sb("kblk", [128, 2 * NB], BF16, at=RA + 8192 + 16384)
    vblk = sb("vblk", [128, 8 * 512], BF16, at=RA + 8192 + 16384 + 4096)
    ex = [sb("ex%d" % i, [128, 512], F32, at=RA + 36864 + i * 2048) for i in range(3)]
    sq2 = sb("sq2", [128, 512], BF16, at=RA + 36864 + 6144)
    kT = sb("kT", [128, 2 * 8192], BF16, at=R0)
    Vt = sb("Vt", [128, 64 * 512], BF16, at=R0 + 32768)
    qT = sb("qT", [128, DC * NB], BF16, at=R0 + 98304)
    oT = sb("oT", [128, DC * NB], BF16, at=R0 + 114688)
    pr = [sb("pr%d" % i, [128, 512], BF16, at=R0 + 131072 + i * 1024) for i in range(8)]
    rc = [ft[0], ft[1]]
    ZT = sb("ZT", [128, 16 * NB], BF16, at=RA)
    zw = sb("zw", [128, 8 * 2048], BF16, at=R0)
    Ztile = [sb("Ztile%d" % i, [128, 4 * 2048], BF16, at=R0 + i * 16384) for i in range(2)]
    Ytile = [sb("Ytile%d" % i, [128, 4 * 2048], BF16, at=R0 + 32768 + i * 16384) for i in range(2)]
    mtile = [sb("mtile%d" % i, [128, 4 * 3 * 128], BF16, at=R0 + 65536 + i * 3072) for i in range(2)]
    xseq = sb("xseq", [128, DC * 4096], F32, at=R0)
    assert R0 + 131072 + 2 * 4096 <= 229376, R0
    Ykh = [sb("Ykh%d" % i, [128, 2048], BF16, at=R0 + 131072 + i * 4096) for i in range(2)]
    stack = contextlib.ExitStack()
    ps = [stack.enter_context(nc.psum_tensor("ps%d" % i, [128, 512], F32)) for i in range(8)]

    def dma(q, out, in_, reads, writes, slot):
        return P.emit(q, lambda e, o=out, i=in_: e.dma_start(out=o, in_=i), reads, writes, slot=slot)

    def seq_of(blk):
        return blk // 2 if blk < 8 else 4

    def mcol(l, j, t, c, b):
        return ((l * 72) + j * 24 + t * 8 + c) * 5 + b

    def gcol(l, j, c, b):
        return ((l * 3 + j) * 8 + c) * 5 + b

    wcnt = [0]

    def wchunk(loads, ncols):
        i = wcnt[0]
        wcnt[0] += 1
        s2, s3 = i % 2, i % 3
        N = len(loads)
        for n, (dfn, src) in enumerate(loads):
            wk = [("ws%d" % s2, q) for q in range(4) if q % N == n]
            dma("sync", dfn(ws[s2]), src, [], wk, slot="ws%d_%d" % (s2, n))
        P.emit("pool", lambda e, a=wb[s3], b=ws[s2], n=ncols: e.tensor_copy(out=a[:, 0:n], in_=b[:, 0:n]),
               [("ws%d" % s2, q) for q in range(4)], [("wb%d" % s3, q) for q in range(4)])
        return s3

    PASSES = [(l_, s_) for l_ in range(L) for s_ in range(2)]

    def precast_ops(l, s):
        pi = PASSES.index((l, s))
        ops = []
        for jc in range(JC):
            widx = (l * 2 + s) * JC + jc
            for h in range(2):
                ops.append((wis[widx].rearrange("p (k c) -> p k c", k=8)[:, :, h * 128:(h + 1) * 128],
                            w_in[l, s, :, h * FF + jc * 128:h * FF + (jc + 1) * 128].rearrange("(k p) c -> p k c", p=128),
                            ("wis", widx, h)))
        for f in range(8):
            widx = (l * 2 + s) * 8 + f
            for h in range(2):
                ops.append((wos[widx].rearrange("p (k c) -> p k c", k=JC)[:, h * 11:(h + 1) * 11, :],
                            w_out[l, s, h * 1408:(h + 1) * 1408, f * 128:(f + 1) * 128].rearrange("(k p) c -> p k c", p=128),
                            ("wos", widx, h)))
        out = []
        for n, (o, i_, key) in enumerate(ops):
            wk = [key] + ([("wlast", l, s)] if n == len(ops) - 1 else [])
            out.append(lambda o=o, i_=i_, wk=wk, pi=pi: dma("pool", o, i_, [], wk, slot="pc%d" % (pi % 2)))
        return out

    for (t, src, nm) in [(gainT, gain_in, "gainT"), (adabT, adab_in, "adabT"), (qkg, qkg_in, "qkg"),
                         (siluc, cT_in, "siluc"), (ident, ident_in, "ident"), (perm, perm_in, "perm"),
                         (ccsc_f, ccsc_in, "ccsc_f")]:
        dma("sync", t[:], src[:, :], [], [(nm, 0)], slot="init_" + nm)
    P.emit("dve", lambda e: e.memset(epst[:], EPS), [], [("epst", 0)])
    P.emit("dve", lambda e: e.memset(ones_bf[:], 1.0), [], [("ones_bf", 0)])
    P.emit("dve", lambda e: e.memset(ones_f[:], 1.0), [], [("ones_f", 0)])
    P.emit("dve", lambda e: e.memset(bones_bf[:], 0.0), [], [("bones_bf", 0)])
    P.emit("dve", lambda e: e.memset(bones_bf[0:64, 0:64], 1.0 / 64), [("bones_bf", 0)], [("bones_bf", 1)])
    P.emit("dve", lambda e: e.memset(bones_bf[64:128, 64:128], 1.0 / 64), [("bones_bf", 1)], [("bones_bf", 2)])
    P.emit("dve", lambda e: e.tensor_copy(out=ccsc[:], in_=ccsc_f[:]), [("ccsc_f", 0)], [("ccsc", 0)])
    P.emit("act", lambda e: e.activation(out=siluc_bf[:], in_=siluc[:], func=AF.Silu), [("siluc", 0)], [("siluc", 1)])
    dma("pool", bdp[:], bd_p_in.rearrange("p a n -> p (a n)"), [], [("bdp", 0)], slot="init_bdp")
    dma("pool", bds[:], bd_s_in.rearrange("p a n -> p (a n)"), [], [("bds", 0)], slot="init_bds")
    CONST_R = [("epst", 0), ("ones_bf", 0), ("bones_bf", 2), ("ccsc", 0), ("siluc", 1), ("ident", 0), ("perm", 0),
               ("gainT", 0), ("adabT", 0), ("qkg", 0), ("bdp", 0), ("bds", 0)]

    def adaln():
        for l in range(L):
            for hc in range(36):
                for _ in range(3):
                    if pc0:
                        pc0.pop(0)()
                wi = wchunk([(lambda t: t[:, 0:2048].rearrange("p (k c) -> p k c", k=8),
                              ada_w[l, :, hc * 256:(hc + 1) * 256].rearrange("(k p) c -> p k c", p=128))], 2048)
                wv = wb[wi][:, 0:2048].rearrange("p (k c) -> p k c", k=8)
                for mi in range(2):
                    m = hc * 2 + mi
                    bk = m % 2
                    for kc in range(8):
                        P.emit("pe", lambda e, o=ps[bk][:, m * 5:m * 5 + 5], w=wv[:, kc, mi * 128:(mi + 1) * 128],
                               r=siluc_bf[:, kc * 5:kc * 5 + 5], st=(kc == 0), sp=(kc == 7):
                               e.matmul(o, w, r, start=st, stop=sp),
                               [("wb%d" % wi, 0), ("wb%d" % wi, 1), ("wb%d" % wi, 2), ("wb%d" % wi, 3), ("siluc", 1)], [("ps", bk)])
                    P.emit("act", lambda e, o=modT[:, (l * 72 + m) * 5:(l * 72 + m) * 5 + 5],
                           i=ps[bk][:, m * 5:m * 5 + 5], b=adabT[:, l * 72 + m:l * 72 + m + 1]:
                           e.activation(out=o, in_=i, func=AF.Identity, bias=b, scale=1.0),
                           [("ps", bk), ("adabT", 0)], [("modT", l)])
        for l in range(L):
            for j in range(3):
                for c in range(8):
                    P.emit("dve", lambda e, o=gsT[:, gcol(l, j, c, 0):gcol(l, j, c, 0) + 5],
                           i=modT[:, mcol(l, j, 1, c, 0):mcol(l, j, 1, c, 0) + 5],
                           g=gainT[:, (l * 3 + j) * 8 + c:(l * 3 + j) * 8 + c + 1]:
                           e.tensor_scalar(out=o, in0=i, scalar1=1.0, scalar2=g, op0=ALU.add, op1=ALU.mult),
                           [("modT", l), ("gainT", 0)], [("gsT", 0)])
                    P.emit("dve", lambda e, o=hgT[:, gcol(l, j, c, 0):gcol(l, j, c, 0) + 5],
                           i=modT[:, mcol(l, j, 2, c, 0):mcol(l, j, 2, c, 0) + 5], f=(1.0 if j == 1 else 0.5):
                           e.tensor_scalar(out=o, in0=i, scalar1=f, scalar2=None, op0=ALU.mult),
                           [("modT", l)], [("hgT", 0)])

    evq = [0]

    def evac(out, in_, reads, writes):
        evq[0] += 1
        if evq[0] % 2:
            return P.emit("act", lambda e: e.activation(out=out, in_=in_, func=AF.Copy), reads, writes)
        return P.emit("dve", lambda e: e.tensor_copy(out=out, in_=in_), reads, writes)

    pbank = [0]

    def nextbank(lo=4, n=4):
        pbank[0] += 1
        return lo + pbank[0] % n

    def ingest(blk):
        stg = xb[1][:, :].rearrange("p (t f) -> p t f", t=8)
        dma("sync", stg, x_in[blk * NB:(blk + 1) * NB, :].rearrange("(t p) f -> p t f", p=128), [], [("xb1", 0)], slot="xb1")
        dst = xb[0][:, :].rearrange("p (c n) -> p c n", c=8)
        for c in range(8):
            for th in range(2):
                b = nextbank()
                for t4 in range(4):
                    tcn = th * 4 + t4
                    P.emit("pe", lambda e, o=ps[b][:, t4 * 128:(t4 + 1) * 128], i=stg[:, tcn, c * 128:(c + 1) * 128]:
                           e.transpose(o, i, ident[:]), [("xb1", 0)], [("ps", b)])
                evac(dst[:, c, th * 512:(th + 1) * 512], ps[b][:, :], [("ps", b)], [("xb0", c)])
        dma("sync", xs[blk], xb[0][:, :], [("xb0", c) for c in range(8)], [("xs", blk)], slot="st_xb0")

    def egress(blk):
        dma("sync", xb[0][:, :], xs[blk], [("xs", blk)], [("xb0", c) for c in range(8)], slot="xb0")
        src = xb[0][:, :].rearrange("p (c n) -> p c n", c=8)
        stg = xb[1][:, :].rearrange("p (t f) -> p t f", t=8)
        for tcn in range(8):
            for fh in range(2):
                b = nextbank()
                for c4 in range(4):
                    c = fh * 4 + c4
                    P.emit("pe", lambda e, o=ps[b][:, c4 * 128:(c4 + 1) * 128], i=src[:, c, tcn * 128:(tcn + 1) * 128]:
                           e.transpose(o, i, ident[:]), [("xb0", c)], [("ps", b)])
                evac(stg[:, tcn, fh * 512:(fh + 1) * 512], ps[b][:, :], [("ps", b)], [("xb1", tcn)])
        dma("sync", y_out[blk * NB:(blk + 1) * NB, :].rearrange("(t p) f -> p t f", p=128), stg,
            [("xb1", t) for t in range(8)], [("y", blk)], slot="st_xb1")

    def modnorm(xt, xname, l, j, b):
        xv = xt[:, :].rearrange("p (c n) -> p c n", c=8)
        hv = hT[:, :].rearrange("p (c n) -> p c n", c=8)
        for tt in range(NB // 512):
            sl = slice(tt * 512, (tt + 1) * 512)
            bank = tt % 2
            for c in range(8):
                P.emit("act", lambda e, o=sq[c % 2][:, :], i=xv[:, c, sl]: e.activation(out=o, in_=i, func=AF.Square),
                       [(xname, c)], [("sq%d" % (c % 2), 0)])
                P.emit("pe", lambda e, o=ps[bank][:, :], r=sq[c % 2][:, :], st=(c == 0), sp=(c == 7):
                       e.matmul(o, ones_bf[:], r, start=st, stop=sp), [("sq%d" % (c % 2), 0)], [("ps", bank)])
            rs = ft[tt]
            P.emit("act", lambda e, o=rs[:, :], i=ps[bank][:, :]: e.activation(out=o, in_=i, func=AF.Sqrt, bias=epst[:, 0:1], scale=1.0 / D),
                   [("ps", bank)], [("ft%d" % tt, 0)])
            P.emit("dve", lambda e, o=rs[:, :]: e.reciprocal(out=o, in_=o), [("ft%d" % tt, 0)], [("ft%d" % tt, 0)])
            for c in range(8):
                tmp = ft[2 + c % 2]
                P.emit("dve", lambda e, o=tmp[:, :], a=xv[:, c, sl], r=rs[:, :]: e.tensor_tensor(out=o, in0=a, in1=r, op=ALU.mult),
                       [(xname, c), ("ft%d" % tt, 0)], [("ft%d" % (2 + c % 2), 0)])
                P.emit("act", lambda e, o=hv[:, c, sl], i=tmp[:, :], s=gsT[:, gcol(l, j, c, b):gcol(l, j, c, b) + 1],
                       bb=modT[:, mcol(l, j, 0, c, b):mcol(l, j, 0, c, b) + 1]:
                       e.activation(out=o, in_=i, func=AF.Identity, bias=bb, scale=s),
                       [("ft%d" % (2 + c % 2), 0)], [("hT", c, tt)])

    def ffn_load_stats(blk, l, s, xi):
        xt, xname = xb[xi], "xb%d" % xi
        dma("sync", xt[:, :], xs[blk], [("xs", blk)], [(xname, c) for c in range(8)], slot=xname)
        xv = xt[:, :].rearrange("p (c n) -> p c n", c=8)
        accs, accn = [ft[2], ft[3]], ["ft2", "ft3"]
        tmp = sq[0][:, :].bitcast(F32) if False else None
        for tt in range(NB // 512):
            sl = slice(tt * 512, (tt + 1) * 512)
            acc, an = accs[tt], accn[tt]
            for c in range(8):
                if c == 0:
                    P.emit("pool", lambda e, o=acc[:, :], a_=xv[:, c, sl]: e.tensor_tensor(out=o, in0=a_, in1=a_, op=ALU.mult),
                           [(xname, c)], [(an, 0)])
                else:
                    P.emit("pool", lambda e, o=sqf[:, :], a_=xv[:, c, sl]: e.tensor_tensor(out=o, in0=a_, in1=a_, op=ALU.mult),
                           [(xname, c)], [("sqf", 0)])
                    P.emit("pool", lambda e, o=acc[:, :], t_=sqf[:, :]: e.tensor_tensor(out=o, in0=o, in1=t_, op=ALU.add),
                           [(an, 0), ("sqf", 0)], [(an, 0)])

    def ffn_stats2(blk, l, s, xi):
        accs, accn = [ft[2], ft[3]], ["ft2", "ft3"]
        for tt in range(NB // 512):
            acc, an = accs[tt], accn[tt]
            bank = tt % 2
            P.emit("pe", lambda e, o=ps[bank][:, :], r=acc[:, :]: e.matmul(o, ones_f[:], r, start=True, stop=True),
                   [(an, 0)], [("ps", bank)])
            rs = ft[tt]
            P.emit("act", lambda e, o=rs[:, :], i=ps[bank][:, :]: e.activation(out=o, in_=i, func=AF.Sqrt, bias=epst[:, 0:1], scale=1.0 / D),
                   [("ps", bank)], [("ft%d" % tt, 0)])
            P.emit("dve", lambda e, o=rs[:, :]: e.reciprocal(out=o, in_=o), [("ft%d" % tt, 0)], [("ft%d" % tt, 0)])

    def ffn_apply(blk, l, s, xi):
        b = seq_of(blk)
        j = 0 if s == 0 else 2
        xt, xname = xb[xi], "xb%d" % xi
        xv = xt[:, :].rearrange("p (c n) -> p c n", c=8)
        hv = hT[:, :].rearrange("p (c n) -> p c n", c=8)
        for tt in range(NB // 512):
            sl = slice(tt * 512, (tt + 1) * 512)
            rs = ft[tt]
            for c in range(8):
                tmp = ft[2 + c % 2]
                P.emit("dve", lambda e, o=tmp[:, :], a_=xv[:, c, sl], r=rs[:, :]: e.tensor_tensor(out=o, in0=a_, in1=r, op=ALU.mult),
                       [(xname, c), ("ft%d" % tt, 0)], [("ft%d" % (2 + c % 2), 0)])
                P.emit("act", lambda e, o=hv[:, c, sl], i=tmp[:, :], s_=gsT[:, gcol(l, j, c, b):gcol(l, j, c, b) + 1],
                       bb=modT[:, mcol(l, j, 0, c, b):mcol(l, j, 0, c, b) + 1]:
                       e.activation(out=o, in_=i, func=AF.Identity, bias=bb, scale=s_),
                       [("ft%d" % (2 + c % 2), 0)], [("hT", c, tt)])

    def wbf(dram_ap, ncols, key):
        i = wcnt[0]
        wcnt[0] += 1
        s3 = i % 3
        dma("sync", wb[s3][:, 0:ncols], dram_ap, key, [("wb%d" % s3, q) for q in range(4)], slot="wbl%d" % s3)
        return s3

    def ffn_a(blk, l, s, xi, first, hook=None):
        hv = hT[:, :].rearrange("p (c n) -> p c n", c=8)
        av = act[:, :].rearrange("p (j n) -> p j n", j=JC)
        for jc in range(JC):
            if jc == 3 and hook is not None:
                hook[0]()
            if jc == 14 and hook is not None:
                hook[1]()
            widx = (l * 2 + s) * JC + jc
            if first:
                wi = wchunk([
                    (lambda t: t[:, 0:2048].rearrange("p (k c) -> p k c", k=8)[:, :, 0:128],
                     w_in[l, s, :, jc * 128:(jc + 1) * 128].rearrange("(k p) c -> p k c", p=128)),
                    (lambda t: t[:, 0:2048].rearrange("p (k c) -> p k c", k=8)[:, :, 128:256],
                     w_in[l, s, :, FF + jc * 128:FF + (jc + 1) * 128].rearrange("(k p) c -> p k c", p=128))], 2048)
                dma("sync", wis[widx], wb[wi][:, 0:2048], [("wb%d" % wi, 0), ("wb%d" % wi, 1), ("wb%d" % wi, 2), ("wb%d" % wi, 3)], [("wis", widx)], slot="st_wb%d" % wi)
            else:
                wi = wbf(wis[widx], 2048, [("wis", widx, 0), ("wis", widx, 1), ("wlast", l, s)])
            wv = wb[wi][:, 0:2048].rearrange("p (k c) -> p k c", k=8)
            for tt in range(NB // 512):
                sl = slice(tt * 512, (tt + 1) * 512)
                bg, bu = 2 + tt, 4 + tt
                for kc in range(8):
                    P.emit("pe", lambda e, o=ps[bg][:, :], w=wv[:, kc, 0:128], r=hv[:, kc, sl], st=(kc == 0), sp=(kc == 7):
                           e.matmul(o, w, r, start=st, stop=sp), [("wb%d" % wi, 0), ("wb%d" % wi, 1), ("wb%d" % wi, 2), ("wb%d" % wi, 3), ("hT", kc, tt)], [("ps", bg)])
                for kc in range(8):
                    P.emit("pe", lambda e, o=ps[bu][:, :], w=wv[:, kc, 128:256], r=hv[:, kc, sl], st=(kc == 0), sp=(kc == 7):
                           e.matmul(o, w, r, start=st, stop=sp), [("wb%d" % wi, 0), ("wb%d" % wi, 1), ("wb%d" % wi, 2), ("wb%d" % wi, 3), ("hT", kc, tt)], [("ps", bu)])
                sg = ft[4 + tt]
                P.emit("act", lambda e, o=sg[:, :], i=ps[bg][:, :]: e.activation(out=o, in_=i, func=AF.Silu),
                       [("ps", bg)], [("ft%d" % (4 + tt), 0)])
                P.emit("dve", lambda e, o=av[:, jc, sl], a=ps[bu][:, :], g=sg[:, :]: e.tensor_tensor(out=o, in0=a, in1=g, op=ALU.mult),
                       [("ps", bu), ("ft%d" % (4 + tt), 0)], [("act", jc, tt)])

    def ffn_b(blk, l, s, xi, first):
        b = seq_of(blk)
        j = 0 if s == 0 else 2
        xt, xname = xb[xi], "xb%d" % xi
        xv = xt[:, :].rearrange("p (c n) -> p c n", c=8)
        av = act[:, :].rearrange("p (j n) -> p j n", j=JC)
        for f in range(8):
            widx = (l * 2 + s) * 8 + f
            if first:
                wi = wchunk([(lambda t: t[:, 0:2816].rearrange("p (k c) -> p k c", k=JC),
                              w_out[l, s, :, f * 128:(f + 1) * 128].rearrange("(k p) c -> p k c", p=128))], 2816)
                dma("sync", wos[widx], wb[wi][:, 0:2816], [("wb%d" % wi, 0), ("wb%d" % wi, 1), ("wb%d" % wi, 2), ("wb%d" % wi, 3)], [("wos", widx)], slot="st_wb%d" % wi)
            else:
                wi = wbf(wos[widx], 2816, [("wos", widx, 0), ("wos", widx, 1), ("wlast", l, s)])
            wv = wb[wi][:, 0:2816].rearrange("p (k c) -> p k c", k=JC)
            for tt in range(NB // 512):
                sl = slice(tt * 512, (tt + 1) * 512)
                bo = 6 + tt
                for jc in range(JC):
                    P.emit("pe", lambda e, o=ps[bo][:, :], w=wv[:, jc, :], r=av[:, jc, sl], st=(jc == 0), sp=(jc == JC - 1):
                           e.matmul(o, w, r, start=st, stop=sp), [("wb%d" % wi, 0), ("wb%d" % wi, 1), ("wb%d" % wi, 2), ("wb%d" % wi, 3), ("act", jc, tt)], [("ps", bo)])
                P.emit("dve", lambda e, o=xv[:, f, sl], a=ps[bo][:, :], g=hgT[:, gcol(l, j, f, b):gcol(l, j, f, b) + 1]:
                       e.scalar_tensor_tensor(out=o, in0=a, scalar=g, in1=o, op0=ALU.mult, op1=ALU.add),
                       [("ps", bo), (xname, f)], [(xname, f)])
        dma("act", xs[blk], xt[:, :], [(xname, c) for c in range(8)], [("xs", blk)], slot="sta_" + xname)

    xi_state = [0]

    def ffn_pass(l, s):
        order = [8, 9, 10, 11] + list(range(8))
        xi = xi_state[0]
        pi = PASSES.index((l, s))
        nxt = precast_ops(*PASSES[pi + 1]) if pi + 1 < len(PASSES) else []
        per = (len(nxt) + 3) // 4
        ffn_load_stats(order[0], l, s, xi)
        ffn_stats2(order[0], l, s, xi)
        ffn_apply(order[0], l, s, xi)
        for n, blk in enumerate(order):
            pcs = nxt[n * per:(n + 1) * per] if n < 4 else []
            if n + 1 < len(order):
                def h0(nb=order[n + 1], x2=xi ^ 1, pcs=pcs):
                    for fn_ in pcs:
                        fn_()
                    ffn_load_stats(nb, l, s, x2)
                ffn_a(blk, l, s, xi, False, hook=(h0, (lambda nb=order[n + 1], x2=xi ^ 1: ffn_stats2(nb, l, s, x2))))
                ffn_apply(order[n + 1], l, s, xi ^ 1)
            else:
                ffn_a(blk, l, s, xi, False)
            ffn_b(blk, l, s, xi, False)
            xi ^= 1
        xi_state[0] = xi

    def attn1(blk, l):
        a = l // 2
        b = seq_of(blk)
        rtype = (blk % 2) if blk < 8 else 2 + (blk - 8)
        xt, xname = xb[1], "xb1"
        dma("sync", xt[:, :], xs[blk], [("xs", blk)], [(xname, c) for c in range(8)], slot=xname)
        dma("sync", ropeb[:, :], rope_in[rtype].rearrange("p a n -> p (a n)"), [], [("ropeb", 0)], slot="ropeb")
        modnorm(xt, xname, l, 1, b)
        hv = hT[:, :].rearrange("p (c n) -> p c n", c=8)
        qv = qblk[:, :].rearrange("p (c n) -> p c n", c=8)
        kv = kblk[:, :].rearrange("p (c n) -> p c n", c=2)
        tiles = [(ci, tt) for ci in range(10) for tt in range(2)]
        x1b, x1n = [ft[0], ft[1], ft[2]], ["ft0", "ft1", "ft2"]
        sqb, sqn = [sq[0], sq[1], sq2], ["sq0", "sq1", "sq2"]
        rsb, rsn = [ft[3], ft[4]], ["ft3", "ft4"]
        t1b, t1n = [ft[5], ex[0]], ["ft5", "ex0"]
        t2b, t2n = [ex[1], ex[2]], ["ex1", "ex2"]
        wstate = {}

        def stage_a(i):
            ci, tt = tiles[i]
            if tt == 0:
                if ci < 8:
                    aa, bb = ci // 4, ci % 4
                    cl = 512 * aa + 64 * bb
                    ch = cl + 256
                    loads = [(lambda t: t[:, 0:1024].rearrange("p (k c) -> p k c", k=8)[:, :, 0:64],
                              w_qkv[a, :, cl:cl + 64].rearrange("(k p) c -> p k c", p=128)),
                             (lambda t: t[:, 0:1024].rearrange("p (k c) -> p k c", k=8)[:, :, 64:128],
                              w_qkv[a, :, ch:ch + 64].rearrange("(k p) c -> p k c", p=128))]
                else:
                    c0 = 1024 + (ci - 8) * 128
                    loads = [(lambda t: t[:, 0:1024].rearrange("p (k c) -> p k c", k=8),
                              w_qkv[a, :, c0:c0 + 128].rearrange("(k p) c -> p k c", p=128))]
                wstate[ci] = wchunk(loads, 1024)
            wi = wstate[ci]
            gcolq = a * 2 if ci < 8 else a * 2 + 1
            wv = wb[wi][:, 0:1024].rearrange("p (k c) -> p k c", k=8)
            sl = slice(tt * 512, (tt + 1) * 512)
            bq = i % 3
            for kc in range(8):
                P.emit("pe", lambda e, o=ps[bq][:, :], w=wv[:, kc, :], r=hv[:, kc, sl], st=(kc == 0), sp=(kc == 7):
                       e.matmul(o, w, r, start=st, stop=sp), [("wb%d" % wi, 0), ("wb%d" % wi, 1), ("wb%d" % wi, 2), ("wb%d" % wi, 3), ("hT", kc, tt)], [("ps", bq)])
            x1 = x1b[i % 3]
            P.emit("act", lambda e, o=x1[:, :], i_=ps[bq][:, :], g=qkg[:, gcolq:gcolq + 1]:
                   e.activation(out=o, in_=i_, func=AF.Identity, scale=g), [("ps", bq)], [(x1n[i % 3], 0)])
            P.emit("act", lambda e, o=sqb[i % 3][:, :], i_=ps[bq][:, :]: e.activation(out=o, in_=i_, func=AF.Square),
                   [("ps", bq)], [(sqn[i % 3], 0)])

        def stage_b(i):
            bm, br = 3 + i % 2, 5 + i % 3
            P.emit("pe", lambda e, o=ps[bm][:, :], r=sqb[i % 3][:, :]: e.matmul(o, bones_bf[:], r, start=True, stop=True),
                   [(sqn[i % 3], 0)], [("ps", bm)])
            P.emit("pe", lambda e, o=ps[br][:, :], r=x1b[i % 3][:, :]: e.matmul(o, perm[:], r, start=True, stop=True),
                   [(x1n[i % 3], 0)], [("ps", br)])
            rs = rsb[i % 2]
            P.emit("act", lambda e, o=rs[:, :], i_=ps[bm][:, :]: e.activation(out=o, in_=i_, func=AF.Sqrt, bias=epst[:, 0:1], scale=1.0),
                   [("ps", bm)], [(rsn[i % 2], 0)])

        def stage_c(i):
            ci, tt = tiles[i]
            sl = slice(tt * 512, (tt + 1) * 512)
            br = 5 + i % 3
            rs, x1, t1, t2 = rsb[i % 2], x1b[i % 3], t1b[i % 2], t2b[i % 2]
            P.emit("dve", lambda e, o=rs[:, :]: e.reciprocal(out=o, in_=o), [(rsn[i % 2], 0)], [(rsn[i % 2], 0)])
            P.emit("dve", lambda e, o=t1[:, :], a_=x1[:, :], c=ropeb[:, sl]: e.tensor_tensor(out=o, in0=a_, in1=c, op=ALU.mult),
                   [(x1n[i % 3], 0), ("ropeb", 0)], [(t1n[i % 2], 0)])
            P.emit("dve", lambda e, o=t2[:, :], a_=ps[br][:, :], c=ropeb[:, NB + tt * 512:NB + (tt + 1) * 512]:
                   e.tensor_tensor(out=o, in0=a_, in1=c, op=ALU.mult), [("ps", br), ("ropeb", 0)], [(t2n[i % 2], 0)])
            P.emit("dve", lambda e, o=t1[:, :], a_=t1[:, :], c=t2[:, :]: e.tensor_tensor(out=o, in0=a_, in1=c, op=ALU.add),
                   [(t1n[i % 2], 0), (t2n[i % 2], 0)], [(t1n[i % 2], 0)])
            dst = qv[:, ci, sl] if ci < 8 else kv[:, ci - 8, sl]
            dkey = ("qblk", ci) if ci < 8 else ("kblk", ci - 8)
            P.emit("dve", lambda e, o=dst, a_=t1[:, :], c=rs[:, :]: e.tensor_tensor(out=o, in0=a_, in1=c, op=ALU.mult),
                   [(t1n[i % 2], 0), (rsn[i % 2], 0)], [dkey])

        nt = len(tiles)
        for step in range(nt + 2):
            if step < nt:
                stage_a(step)
            if 0 <= step - 1 < nt:
                stage_b(step - 1)
            if 0 <= step - 2 < nt:
                stage_c(step - 2)
        wi = wchunk([(lambda t: t[:, 0:2048].rearrange("p (k c) -> p k c", k=8),
                      w_qkv[a, :, 1280:1536].rearrange("(k p) c -> p k c", p=128))], 2048)
        wv = wb[wi][:, 0:2048].rearrange("p (k c) -> p k c", k=8)
        vv = vblk[:, :].rearrange("p (t g d) -> p t g d", t=8, g=4)
        P.emit("pool", lambda e, o=vv[:, :, :, 64:128]: e.memset(o, 1.0), [], [("vblk", "ones")])
        for tcn in range(8):
            bv = nextbank(2, 2)
            for kc in range(8):
                P.emit("pe", lambda e, o=ps[bv][:, 0:256], w=hv[:, kc, tcn * 128:(tcn + 1) * 128], r=wv[:, kc, :], st=(kc == 0), sp=(kc == 7):
                       e.matmul(o, w, r, start=st, stop=sp), [("wb%d" % wi, 0), ("wb%d" % wi, 1), ("wb%d" % wi, 2), ("wb%d" % wi, 3), ("hT", kc, tcn // 4)], [("ps", bv)])
            evac(vv[:, tcn, :, 0:64], ps[bv][:, 0:256].rearrange("p (g d) -> p g d", g=4), [("ps", bv)], [("vblk", tcn)])
        dma("sync", qs[blk], qblk[:, :], [("qblk", c) for c in range(8)], [("qs", blk)], slot="st_qblk")
        if blk < 8:
            sq_, hf = blk // 2, blk % 2
            kdst = kp[sq_].rearrange("p (c n) -> p c n", c=2)[:, :, hf * NB:(hf + 1) * NB]
            vdst = vp[sq_, hf * NB:(hf + 1) * NB, :].rearrange("(t p) d -> p t d", p=128)
            kkey, vkey = ("kp", sq_, hf), ("vp", sq_, hf)
        else:
            sb_ = blk - 8
            kdst = ko.rearrange("p (c n) -> p c n", c=2)[:, :, sb_ * NB:(sb_ + 1) * NB]
            vdst = None
            kkey, vkey = ("ko", sb_), ("vo", sb_)
        dma("sync", kdst, kv, [("kblk", 0), ("kblk", 1)], [kkey], slot="st_kblk")
        vrd = [("vblk", t) for t in range(8)] + [("vblk", "ones")]
        v3 = vblk[:, :].rearrange("p (t d) -> p t d", t=8)
        if vdst is not None:
            dma("sync", vdst, v3, vrd, [vkey], slot="st_vblk0")
        else:
            for i in range(2):
                dma("sync", vo[i][sb_ * NB:(sb_ + 1) * NB, :].rearrange("(t p) d -> p t d", p=128), v3[:, :, i * 256:(i + 1) * 256],
                    vrd, [("vo", sb_, i)], slot="st_vblk%d" % i)

    def gather(kind):
        RG = [[0, 1], [2, 3], [4, 5], [6, 7]]
        if kind == "kv":
            P.emit("pool", lambda e: e.collective_compute("AllGather", ALU.bypass, replica_groups=RG,
                                                          ins=[ko_t.ap().opt()], outs=[kg_t.ap().opt()]),
                   [("ko", i) for i in range(4)], [("kg", 0)], slot="cc_k", inc=1)
            for j in range(2):
                P.emit("pool", lambda e, j=j: e.collective_compute("AllGather", ALU.bypass, replica_groups=RG,
                                                                   ins=[vo_t[j].ap().opt()], outs=[vg_t[j].ap().opt()]),
                       [("vo", i, j) for i in range(4)], [("vg", j)], slot="cc_v%d" % j, inc=1)
        else:
            for j in range(8):
                P.emit("pool", lambda e, j=j: e.collective_compute("AllGather", ALU.bypass, replica_groups=RG,
                                                                   ins=[zo_t[j].ap().opt()], outs=[zg_t[j].ap().opt()]),
                       [("zo", i, j) for i in range(4)], [("zg", j)], slot="cc_z%d" % j, inc=1)

    def attn2(blk, l, load_kv):
        a = l // 2
        b = seq_of(blk)
        S = 2048 if blk < 8 else 8192
        NSC = S // 128
        kTv = kT[:, 0:2 * S].rearrange("p (c n) -> p c n", c=2)
        Vv = Vt[:, 0:NSC * 512].rearrange("p (t d) -> p t d", t=NSC)
        if load_kv:
            if blk < 8:
                sq_ = blk // 2
                dma("sync", kT[:, 0:2 * S], kp[sq_], [("kp", sq_, 0), ("kp", sq_, 1)], [("kT", 0)], slot="kT")
                dma("sync", Vv, vp[sq_].rearrange("(t p) d -> p t d", p=128), [("vp", sq_, 0), ("vp", sq_, 1)], [("Vt", 0), ("Vt", 1)], slot="Vt0")
            else:
                dma("sync", kT[:, :].rearrange("p (c r n) -> p c r n", c=2, r=2), kg.rearrange("(r p) (c n) -> p c r n", r=2, c=2),
                    [("kg", 0)], [("kT", 0)], slot="kT")
                for i in range(2):
                    dma("sync", Vv[:, :, i * 256:(i + 1) * 256], vg[i].rearrange("(t p) d -> p t d", p=128), [("vg", i)], [("Vt", i)], slot="Vt%d" % i)
        dma("sync", qT[:, :], qs[blk], [("qs", blk)], [("qT", 0)], slot="qT")
        qv = qT[:, :].rearrange("p (c n) -> p c n", c=8)
        ov = oT[:, :].rearrange("p (c n) -> p c n", c=8)
        scale = 0.125
        pcnt = 0
        for ci in range(8):
            aa, bb = ci // 4, ci % 4
            heads = [(8 * aa + bb, 0), (8 * aa + 4 + bb, 64)]
            for qt in range(NB // 512):
                sl = slice(qt * 512, (qt + 1) * 512)
                ob = [4 + ((ci * 2 + qt) % 2) * 2, 5 + ((ci * 2 + qt) % 2) * 2]

                def qk(sc):
                    for hi, (h, p0) in enumerate(heads):
                        g = h // 4
                        bs = hi * 2 + sc % 2
                        P.emit("pe", lambda e, o=ps[bs][:, :], w=kTv[p0:p0 + 64, g // 2, sc * 128:(sc + 1) * 128], r=qv[p0:p0 + 64, ci, sl]:
                               e.matmul(o, w, r, start=True, stop=True), [("kT", 0), ("qT", 0)], [("ps", bs)])
                qk(0)
                for sc in range(NSC):
                    if sc + 1 < NSC:
                        qk(sc + 1)
                    for hi, (h, p0) in enumerate(heads):
                        g = h // 4
                        bs = hi * 2 + sc % 2
                        pi = pcnt % 8
                        pcnt += 1
                        P.emit("act", lambda e, o=pr[pi][:, :], i=ps[bs][:, :]: e.activation(out=o, in_=i, func=AF.Exp, scale=scale),
                               [("ps", bs)], [("pr%d" % pi, 0)])
                        P.emit("pe", lambda e, o=ps[ob[hi]][:, :], sc=sc, g=g, r=pr[pi][:, :], st=(sc == 0), sp=(sc == NSC - 1):
                               e.matmul(o, Vt[:, sc * 512 + g * 128:sc * 512 + (g + 1) * 128], r, start=st, stop=sp),
                               [("Vt", 0), ("Vt", 1), ("pr%d" % pi, 0)], [("ps", ob[hi])])
                for hi, (h, p0) in enumerate(heads):
                    r_ = rc[hi]
                    P.emit("dve", lambda e, o=r_[0:64, :], i=ps[ob[hi]][64:128, :]: e.reciprocal(out=o, in_=i),
                           [("ps", ob[hi])], [("ft%d" % hi, 0)])
                    P.emit("dve", lambda e, o=ov[p0:p0 + 64, ci, sl], a=ps[ob[hi]][0:64, :], c=r_[0:64, :]:
                           e.tensor_tensor(out=o, in0=a, in1=c, op=ALU.mult), [("ps", ob[hi]), ("ft%d" % hi, 0)], [("oT", ci, hi, qt)])
        xt, xname = xb[0], "xb0"
        dma("sync", xt[:, :], xs[blk], [("xs", blk)], [(xname, c) for c in range(8)], slot=xname)
        xv = xt[:, :].rearrange("p (c n) -> p c n", c=8)
        for f in range(8):
            wsrc = w_ao[a].rearrange("(a h b p) n -> p a h b n", a=2, h=2, b=4, p=64)
            lds = []
            for h_ in range(2):
                for a_ in range(2):
                    lds.append((lambda t, h_=h_, a_=a_: t[h_ * 64:(h_ + 1) * 64, a_ * 512:(a_ + 1) * 512].rearrange("p (b c) -> p b c", b=4),
                                wsrc[:, a_, h_, :, f * 128:(f + 1) * 128]))
            wi = wchunk(lds, 1024)
            wv = wb[wi][:, 0:1024].rearrange("p (k c) -> p k c", k=8)
            for tt in range(NB // 512):
                sl = slice(tt * 512, (tt + 1) * 512)
                bo = nextbank(0, 4)
                for ci in range(8):
                    P.emit("pe", lambda e, o=ps[bo][:, :], w=wv[:, ci, :], r=ov[:, ci, sl], st=(ci == 0), sp=(ci == 7):
                           e.matmul(o, w, r, start=st, stop=sp), [("wb%d" % wi, 0), ("wb%d" % wi, 1), ("wb%d" % wi, 2), ("wb%d" % wi, 3), ("oT", ci, 0, tt), ("oT", ci, 1, tt)], [("ps", bo)])
                P.emit("dve", lambda e, o=xv[:, f, sl], a=ps[bo][:, :], g=hgT[:, gcol(l, 1, f, b):gcol(l, 1, f, b) + 1]:
                       e.scalar_tensor_tensor(out=o, in0=a, scalar=g, in1=o, op0=ALU.mult, op1=ALU.add),
                       [("ps", bo), (xname, f)], [(xname, f)])
        dma("sync", xs[blk], xt[:, :], [(xname, c) for c in range(8)], [("xs", blk)], slot="st_" + xname)

    def vaug(Vt_, sc, g):
        base = Vt_[:, sc * 320 + g * 64: sc * 320 + g * 64 + 64]
        full = Vt_[:, sc * 320 + g * 64: sc * 320 + 320]
        n = (256 - 64 * g) // 64
        v3 = full.rearrange("p (a d) -> p a d", d=64)
        return v3[:, 0:n + 1:n, :]

    def fnet1(blk, l):
        a = l // 2
        b = seq_of(blk)
        xt, xname = xb[1], "xb1"
        dma("sync", xt[:, :], xs[blk], [("xs", blk)], [(xname, c) for c in range(8)], slot=xname)
        modnorm(xt, xname, l, 1, b)
        hv = hT[:, :].rearrange("p (c n) -> p c n", c=8)
        Zv = ZT[:, :].rearrange("p (c n) -> p c n", c=16)
        for g in range(8):
            for tt in range(NB // 512):
                sl = slice(tt * 512, (tt + 1) * 512)
                for part in range(2):
                    bz = nextbank(2, 4)
                    P.emit("pe", lambda e, o=ps[bz][:, :], w=ccsc[:, part * 128:(part + 1) * 128], r=hv[:, g, sl]:
                           e.matmul(o, w, r, start=True, stop=True), [("hT", g, tt)], [("ps", bz)])
                    evac(Zv[:, part * 8 + g, sl], ps[bz][:, :], [("ps", bz)], [("ZT", part * 8 + g, tt)])
        zv = zw[:, :].rearrange("p (t f) -> p t f", t=8)
        for cg in range(4):
            wi = wchunk([(lambda t: t[:, 0:2048].rearrange("p (k c) -> p k c", k=8),
                          w_fo[a, :, cg * 256:(cg + 1) * 256].rearrange("(k p) c -> p k c", p=128))], 2048)
            wv = wb[wi][:, 0:2048].rearrange("p (k c) -> p k c", k=8)
            for tcn in range(8):
                for part in range(2):
                    bz = nextbank(2, 4)
                    for dc in range(8):
                        P.emit("pe", lambda e, o=ps[bz][:, 0:256], w=Zv[:, part * 8 + dc, tcn * 128:(tcn + 1) * 128], r=wv[:, dc, :], st=(dc == 0), sp=(dc == 7):
                               e.matmul(o, w, r, start=st, stop=sp), [("wb%d" % wi, 0), ("wb%d" % wi, 1), ("wb%d" % wi, 2), ("wb%d" % wi, 3), ("ZT", part * 8 + dc, tcn // 4)], [("ps", bz)])
                    evac(zv[:, tcn, part * 1024 + cg * 256:part * 1024 + (cg + 1) * 256], ps[bz][:, 0:256], [("ps", bz)], [("zw", tcn, part, cg)])
        if blk < 8:
            sq_, hf = blk // 2, blk % 2
            zdst = zp[sq_, hf * NB:(hf + 1) * NB, :].rearrange("(t p) f -> p t f", p=128)
            zkey = ("zp", sq_, hf)
        else:
            sb_ = blk - 8
            zdst = None
        zrd = [("zw", t, p_, c) for t in range(8) for p_ in range(2) for c in range(4)]
        if zdst is not None:
            dma("sync", zdst, zv, zrd, [zkey], slot="st_zw0")
        else:
            for i in range(8):
                dma("sync", zo[i][sb_ * NB:(sb_ + 1) * NB, :].rearrange("(t p) f -> p t f", p=128), zv[:, :, i * 256:(i + 1) * 256],
                    zrd, [("zo", sb_, i)], slot="st_zw%d" % i)

    def fnet2(seq):
        sample = seq == 4
        Q = 64 if sample else 16
        mt_in = mtab_s_in if sample else mtab_p_in
        for grp in range(Q // 4):
            i2 = grp % 2
            Zv = Ztile[i2][:, :].rearrange("p (q f) -> p q f", q=4)
            if sample:
                for i in range(8):
                    src = zg[i].rearrange("(n1 n2) f -> n1 n2 f", n2=64)[:, grp * 4:(grp + 1) * 4, :]
                    dma("sync", Zv[:, :, i * 256:(i + 1) * 256], src, [("zg", i)], [("Ztile%d" % i2, i)], slot="Ztile%d_%d" % (i2, i))
            else:
                src = zp[seq].rearrange("(n1 n2) f -> n1 n2 f", n2=16)[:, grp * 4:(grp + 1) * 4, :]
                dma("sync", Zv, src, [("zp", seq, 0), ("zp", seq, 1)], [("Ztile%d" % i2, i) for i in range(8)], slot="Ztile%d_0" % i2)
            mv = mtile[i2][:, :].rearrange("p (q a n) -> p q a n", q=4, a=3)
            dma("pool", mv, mt_in[grp * 4:(grp + 1) * 4].rearrange("q p a n -> p q a n"), [], [("mtile%d" % i2, 0)], slot="mtile%d" % i2)
            Yv = Ytile[i2][:, :].rearrange("p (q f) -> p q f", q=4)
            for q in range(4):
                for fh in range(2):
                    A_ = Zv[:, q, fh * 512:(fh + 1) * 512]
                    B_ = Zv[:, q, 1024 + fh * 512:1024 + (fh + 1) * 512]
                    br_, bi_ = nextbank(0, 4), nextbank(4, 4)
                    rd = [("Ztile%d" % i2, i) for i in range(8)] + [("mtile%d" % i2, 0)]
                    P.emit("pe", lambda e, o=ps[br_][:, :], w=mv[:, q, 0, :], r=A_: e.matmul(o, w, r, start=True, stop=False), rd, [("ps", br_)])
                    P.emit("pe", lambda e, o=ps[br_][:, :], w=mv[:, q, 1, :], r=B_: e.matmul(o, w, r, start=False, stop=True), rd, [("ps", br_)])
                    P.emit("pe", lambda e, o=ps[bi_][:, :], w=mv[:, q, 1, :], r=A_: e.matmul(o, w, r, start=True, stop=False), rd, [("ps", bi_)])
                    P.emit("pe", lambda e, o=ps[bi_][:, :], w=mv[:, q, 2, :], r=B_: e.matmul(o, w, r, start=False, stop=True), rd, [("ps", bi_)])
                    evac(Yv[:, q, fh * 512:(fh + 1) * 512], ps[br_][:, :], [("ps", br_)], [("Ytile%d" % i2, q, fh, 0)])
                    evac(Yv[:, q, 1024 + fh * 512:1024 + (fh + 1) * 512], ps[bi_][:, :], [("ps", bi_)], [("Ytile%d" % i2, q, fh, 1)])
            ydram = ysq if sample else yp[seq]
            ydst = ydram.rearrange("(k1 n2) f -> k1 n2 f", n2=Q)[:, grp * 4:(grp + 1) * 4, :]
            dma("sync", ydst, Yv, [("Ytile%d" % i2, q, fh, ri) for q in range(4) for fh in range(2) for ri in range(2)],
                [("ysq", grp) if sample else ("yp", seq, grp)], slot="st_Ytile%d" % i2)

    def fnet3(seq, l):
        sample = seq == 4
        Q = 64 if sample else 16
        S = 8192 if sample else 2048
        T_own = 4096 if sample else 2048
        NKH = S // 128
        R = 128 // Q
        blks = [8, 9, 10, 11] if sample else [2 * seq, 2 * seq + 1]
        xv = xseq[:, 0:DC * T_own].rearrange("p (c n) -> p c n", c=8)
        for bi_, blk in enumerate(blks):
            dma("sync", xv[:, :, bi_ * NB:(bi_ + 1) * NB], xs[blk].rearrange("p (c n) -> p c n", c=8), [("xs", blk)], [("xseq", bi_)], slot="xseq%d" % bi_)
        bd = bds if sample else bdp
        NCOL = 64 if sample else 128
        bdv = bd[:, :].rearrange("p (a n) -> p a n", a=2)
        KB = 512 // NCOL
        ydram = ysq if sample else yp[seq]
        yrd = [("ysq", g) for g in range(Q // 4)] if sample else [("yp", seq, g) for g in range(Q // 4)]
        nk2 = NCOL // R
        for khg in range(NKH // KB):
            for kb in range(KB):
                kh = khg * KB + kb
                yi = kh % 2
                dma("sync", Ykh[yi][:, :], ydram[kh * 128:(kh + 1) * 128, :], yrd, [("Ykh%d" % yi, 0)], slot="Ykh%d" % yi)
                for f in range(8):
                    bo = f
                    o = ps[bo][:, kb * NCOL:(kb + 1) * NCOL]
                    P.emit("pe", lambda e, o=o, w=Ykh[yi][:, f * 128:(f + 1) * 128], r=bdv[:, 0, :]: e.matmul(o, w, r, start=True, stop=False),
                           [("Ykh%d" % yi, 0)], [("ps", bo)])
                    P.emit("pe", lambda e, o=o, w=Ykh[yi][:, 1024 + f * 128:1024 + (f + 1) * 128], r=bdv[:, 1, :]: e.matmul(o, w, r, start=False, stop=True),
                           [("Ykh%d" % yi, 0)], [("ps", bo)])
            for f in range(8):
                pv = ps[f][:, :].rearrange("p (m k) -> p m k", k=nk2)
                t0 = khg * KB * R
                xo = xv[:, f, :].rearrange("p (k m) -> p m k", m=128)[:, t0:t0 + KB * R, :]
                P.emit("dve", lambda e, o=xo, a=pv, g=hgT[:, gcol(l, 1, f, seq):gcol(l, 1, f, seq) + 1]:
                       e.scalar_tensor_tensor(out=o, in0=a, scalar=g, in1=o, op0=ALU.mult, op1=ALU.add),
                       [("ps", f)] + [("xseq", i) for i in range(len(blks))], [("xseq", "u", f)])
        for bi_, blk in enumerate(blks):
            dma("sync", xs[blk].rearrange("p (c n) -> p c n", c=8), xv[:, :, bi_ * NB:(bi_ + 1) * NB],
                [("xseq", "u", f) for f in range(8)] + [("xseq", bi_)], [("xs", blk)], slot="st_xseq%d" % bi_)

    pc0 = precast_ops(0, 0)
    adaln()
    assert not pc0
    for blk in range(NBLK):
        ingest(blk)
    sub = 0
    xi = 0
    done = False
    for l in range(L):
        if done:
            break
        ffn_pass(l, 0)
        sub += 1
        if sub >= n_sub:
            break
        if l % 2 == 0:
            for blk in [8, 9, 10, 11]:
                attn1(blk, l)
            gather("kv")
            for sq_ in range(4):
                attn1(2 * sq_, l)
                attn1(2 * sq_ + 1, l)
                attn2(2 * sq_, l, True)
                attn2(2 * sq_ + 1, l, True)
            for i, blk in enumerate([8, 9, 10, 11]):
                attn2(blk, l, True)
        else:
            for blk in [8, 9, 10, 11]:
                fnet1(blk, l)
            gather("z")
            for sq_ in range(4):
                fnet1(2 * sq_, l)
                fnet1(2 * sq_ + 1, l)
                fnet2(sq_)
                fnet3(sq_, l)
            fnet2(4)
            fnet3(4, l)
        sub += 1
        if sub >= n_sub:
            break
        ffn_pass(l, 1)
        sub += 1
        if sub >= n_sub:
            break
    for blk in range(NBLK):
        egress(blk)

    P.finalize()
    keys = P.semkeys()
    sems = {}
    for i, k in enumerate(keys):
        sems[k] = stack.enter_context(nc.semaphore("s%d" % i))
    with stack:
        with nc.Block() as block:
            @block.sync
            def _(e):
                P.replay("sync", e, sems)
                for k in keys:
                    if k[0] == "slot":
                        e.wait_ge(sems[k], P.semreq[("slot", k[1])])

            @block.scalar
            def _(e):
                P.replay("act", e, sems)

            @block.vector
            def _(e):
                P.replay("dve", e, sems)

            @block.gpsimd
            def _(e):
                P.replay("pool", e, sems)

            @block.tensor
            def _(e):
                P.replay("pe", e, sems)
    return nc, P


_CACHE = {}


def _consts(par):
    c = {}
    c["ident"] = np.eye(128, dtype=np.float32)
    pm = np.zeros((128, 128), np.float32)
    for i in range(64):
        pm[2 * i + 1, 2 * i] = -1.0
        pm[2 * i, 2 * i + 1] = 1.0
    c["perm"] = pm
    cd = np.arange(128)[:, None] * np.arange(128)[None, :]
    ang = 2 * np.pi * cd / 128.0
    c["ccsc"] = np.concatenate([np.cos(ang), np.sin(ang)], axis=1).astype(np.float32)
    inv = 10000.0 ** (-np.arange(0, 32, 2, dtype=np.float32) / 32.0)
    rope = np.zeros((6, 128, 2, NB), np.float32)
    for ty in range(6):
        if ty < 2:
            pos = ty * NB + np.arange(NB)
        else:
            pos = par * 4096 + (ty - 2) * NB + np.arange(NB)
        r = (pos // 64).astype(np.float32)
        cc = (pos % 64).astype(np.float32)
        angp = np.concatenate([r[:, None] * inv, cc[:, None] * inv], axis=-1).astype(np.float32)
        for p in range(128):
            pr_ = (p % 64) // 2
            rope[ty, p, 0] = np.cos(angp[:, pr_])
            rope[ty, p, 1] = np.sin(angp[:, pr_])
    c["rope"] = rope
    for nm, Q in (("mtab_p", 16), ("mtab_s", 64)):
        S = 128 * Q
        n1 = np.arange(128)[:, None].astype(np.float64)
        k1 = np.arange(128)[None, :].astype(np.float64)
        mt = np.zeros((Q, 128, 3, 128), np.float32)
        for n2 in range(Q):
            th = 2 * np.pi * (n1 * k1 / 128.0 + n2 * k1 / S)
            mt[n2, :, 0] = np.cos(th)
            mt[n2, :, 1] = -np.sin(th)
            mt[n2, :, 2] = -np.cos(th)
        c[nm] = mt
    bdp = np.zeros((128, 2, 128), np.float64)
    Q, R = 16, 8
    nrm = 1.0 / np.sqrt(2048 * 128.0)
    for kl in range(R):
        for n2 in range(Q):
            for k2 in range(16):
                ph = 2 * np.pi * n2 * k2 / Q
                bdp[kl * Q + n2, 0, kl * 16 + k2] = np.cos(ph) * nrm
                bdp[kl * Q + n2, 1, kl * 16 + k2] = np.sin(ph) * nrm
    c["bd_p"] = bdp.astype(np.float32)
    bds = np.zeros((128, 2, 64), np.float64)
    Q, R = 64, 2
    nrm = 1.0 / np.sqrt(8192 * 128.0)
    for kl in range(R):
        for n2 in range(Q):
            for k2l in range(32):
                k2 = par * 32 + k2l
                ph = 2 * np.pi * n2 * k2 / Q
                bds[kl * Q + n2, 0, kl * 32 + k2l] = np.cos(ph) * nrm
                bds[kl * Q + n2, 1, kl * 32 + k2l] = np.sin(ph) * nrm
    c["bd_s"] = bds.astype(np.float32)
    return c


def kernel(x_prompt, x_sample, c_prompt, c_sample, norm_gain, ada_w, ada_b, ffn_w_in, ffn_w_out,
           attn_w_qkv, attn_q_gain, attn_k_gain, attn_w_o, fnet_w_o, _n_sub=12):
    f = lambda a: np.ascontiguousarray(np.asarray(a, dtype=np.float32))
    x_prompt, x_sample, c_prompt, c_sample = f(x_prompt), f(x_sample), f(c_prompt), f(c_sample)
    norm_gain, ada_w, ada_b = f(norm_gain), f(ada_w), f(ada_b)
    ffn_w_in, ffn_w_out, attn_w_qkv, attn_w_o, fnet_w_o = f(ffn_w_in), f(ffn_w_out), f(attn_w_qkv), f(attn_w_o), f(fnet_w_o)
    qg, kg_ = f(attn_q_gain), f(attn_k_gain)
    key = _n_sub
    if key not in _CACHE:
        _CACHE[key] = build_program(_n_sub)[0]
    nc = _CACHE[key]
    gainT = np.ascontiguousarray(norm_gain.reshape(4, 3, 8, 128).transpose(3, 0, 1, 2).reshape(128, 96))
    adabT = np.ascontiguousarray(ada_b.reshape(4, 72, 128).transpose(2, 0, 1).reshape(128, 288))
    qkg = np.stack([np.tile(qg[0], 2), np.tile(kg_[0], 2), np.tile(qg[1], 2), np.tile(kg_[1], 2)], axis=1).astype(np.float32)
    consts = [_consts(0), _consts(1)]
    in_maps = []
    for c in range(8):
        par = c % 2
        xs_ = np.concatenate([x_prompt[4 * c:4 * c + 4].reshape(8192, D),
                              x_sample[c // 2, par * 4096:(par + 1) * 4096]], axis=0)
        cv = np.concatenate([c_prompt[4 * c:4 * c + 4], c_sample[c // 2:c // 2 + 1]], axis=0)
        cT = np.ascontiguousarray(cv.reshape(5, 8, 128).transpose(2, 1, 0).reshape(128, 40))
        m = {"x": np.ascontiguousarray(xs_), "cT": cT, "gainT": gainT, "adabT": adabT, "qkg": qkg,
             "ada_w": ada_w, "ffn_w_in": ffn_w_in, "ffn_w_out": ffn_w_out, "attn_w_qkv": attn_w_qkv,
             "attn_w_o": attn_w_o, "fnet_w_o": fnet_w_o}
        m.update(consts[par])
        in_maps.append(m)
    res = run_bass_kernel_spmd(nc, in_maps, core_ids=list(range(8)))
    y_prompt = np.zeros((32, 2048, D), np.float32)
    y_sample = np.zeros((4, 8192, D), np.float32)
    for c in range(8):
        y = res.results[c]["y"]
        y_prompt[4 * c:4 * c + 4] = y[0:8192].reshape(4, 2048, D)
        y_sample[c // 2, (c % 2) * 4096:(c % 2 + 1) * 4096] = y[8192:12288]
    return (y_prompt, y_sample)
```
